# Optimizing a Trainium2 kernel written in Bass

```python
import math
import jax
import jax.numpy as jnp
from jax import lax
import numpy as np

D_MODEL = 1024
BATCH = 32
SEQ = 256
DEPTH = 2
DEC_BATCH = 2
DEC_SEQ = 2048
PAST_LEN = 512

GRID_W = 64
EPS = 1e-6
NA_HEADS = 8
NA_HEAD_DIM = 64
NA_WIN_H = 8
NA_WIN_W = 16
NA_KEY_COLS = 2 * NA_WIN_W
NA_Q_BLOCK = 128
DN_HEADS = 4
DN_HEAD_DIM = 128
DN_CONV = 3
DN_CHUNK = 64
GLA_HEADS = 4
GLA_KEY_DIM = 64
GLA_VAL_DIM = 128
GLA_GATE_RANK = 16
GLA_TAU = 16.0
GLA_CHUNK = 32
D_FF = 2816
FFN_CONV = 3

NA_W = NA_HEADS * NA_HEAD_DIM
DN_W = DN_HEADS * DN_HEAD_DIM
GLA_KW = GLA_HEADS * GLA_KEY_DIM
GLA_VW = GLA_HEADS * GLA_VAL_DIM
BRANCH_W = 512
N_BRANCH = 3
PROJ_SIZES = (NA_W, NA_W, NA_W,
              DN_W, DN_W, DN_W, DN_W,
              2 * DN_HEADS, 2 * DN_HEADS,
              GLA_KW, GLA_KW, GLA_VW, GLA_VW,
              2 * GLA_GATE_RANK,
              D_MODEL, D_MODEL, D_MODEL)
PROJ_DIM = sum(PROJ_SIZES)

kernel_name = 'hybrid_na_deltanet_gla_diffusion_step'


def rms_norm(x, g):
    xf = x.astype(jnp.float32)
    y = xf * lax.rsqrt(jnp.mean(xf * xf, axis=-1, keepdims=True) + EPS)
    return (y * g.astype(jnp.float32)).astype(x.dtype)


def l2_norm(x):
    return x * lax.rsqrt(jnp.sum(x * x, axis=-1, keepdims=True) + EPS)


def rev(x):
    return jnp.flip(x, axis=1)


def dw_conv(x, w):
    k = w.shape[0]
    return lax.conv_general_dilated(
        x, w[:, None, :].astype(x.dtype), window_strides=(1,),
        padding=[((k - 1) // 2, k // 2)],
        dimension_numbers=('NWC', 'WIO', 'NWC'),
        feature_group_count=x.shape[-1])


def adaln(cvec, w, b):
    return jax.nn.silu(cvec) @ w + b


def to_chunks(x, c):
    b, t, h = x.shape[:3]
    return jnp.moveaxis(x.reshape(b, t // c, c, h, *x.shape[3:]), 3, 1)


def context_attention(q, k, v):
    B, S, H, d = q.shape
    q_blocks = jnp.moveaxis(q.reshape(B, S // NA_Q_BLOCK, NA_Q_BLOCK, H, d), 1, 0)

    def one_block(q_b):
        s = jnp.einsum('bqhd,bkhd->bhqk', q_b, k).astype(jnp.float32)
        p = jax.nn.softmax(s, axis=-1).astype(v.dtype)
        return jnp.einsum('bhqk,bkhd->bqhd', p, v)

    o = lax.map(one_block, q_blocks)
    return jnp.moveaxis(o, 0, 1).reshape(B, S, H, d)


def neighbourhood_attention(q, k, v, k_ctx, v_ctx, rpb):
    B, T, H, d = q.shape
    rows = T // GRID_W
    kh = min(NA_WIN_H, rows)
    ncb = GRID_W // NA_WIN_W
    r = np.arange(rows)
    key_rows = np.clip(r - kh // 2, 0, rows - kh)[:, None] + np.arange(kh)
    q_cols = np.arange(GRID_W).reshape(ncb, NA_WIN_W)
    blk_start = np.clip(q_cols[:, 0] - NA_WIN_W // 2, 0, GRID_W - NA_KEY_COLS)
    key_cols = blk_start[:, None] + np.arange(NA_KEY_COLS)
    win_start = np.clip(q_cols - NA_WIN_W // 2, 0, GRID_W - NA_WIN_W)
    nk = kh * NA_KEY_COLS
    idx = (key_rows[:, None, :, None] * GRID_W + key_cols[None, :, None, :]).reshape(rows, ncb, nk).astype(np.int32)
    kc_b = key_cols[:, None, :]
    col_ok = (kc_b >= win_start[:, :, None]) & (kc_b < win_start[:, :, None] + NA_WIN_W)
    mask = np.broadcast_to(col_ok[:, :, None, :], (ncb, NA_WIN_W, kh, NA_KEY_COLS)).reshape(ncb, NA_WIN_W, nk)
    dr = key_rows - r[:, None] + NA_WIN_H - 1
    dc = np.clip(kc_b - q_cols[:, :, None], 1 - NA_WIN_W, NA_WIN_W - 1) + NA_WIN_W - 1
    bias = rpb.astype(jnp.float32)[:, dr[:, None, None, :, None], dc[None, :, :, None, :]]
    bias = jnp.moveaxis(bias.reshape(H, rows, ncb, NA_WIN_W, nk), 0, 1)
    q_rows = jnp.moveaxis(q.reshape(B, rows, ncb, NA_WIN_W, H, d), 1, 0)

    def one_row(args):
        q_r, idx_r, bias_r = args
        k_r = k[:, idx_r]
        v_r = v[:, idx_r]
        s_loc = jnp.einsum('bjqhd,bjkhd->bhjqk', q_r, k_r).astype(jnp.float32) + bias_r
        s_loc = jnp.where(mask, s_loc, -jnp.inf)
        s_ctx = jnp.einsum('bjqhd,blhd->bhjql', q_r, k_ctx).astype(jnp.float32)
        p = jax.nn.softmax(jnp.concatenate([s_loc, s_ctx], axis=-1), axis=-1).astype(v.dtype)
        return (jnp.einsum('bhjqk,bjkhd->bjqhd', p[..., :nk], v_r)
                + jnp.einsum('bhjql,blhd->bjqhd', p[..., nk:], v_ctx))

    o = lax.map(one_row, (q_rows, jnp.asarray(idx), bias))
    return jnp.moveaxis(o, 0, 1).reshape(B, T, H, d)


def gated_delta_chunked(q, k, v, g, beta, s0):
    B, T, H, dk = k.shape
    dv = v.shape[-1]
    C = DN_CHUNK
    qc, kc, vc = to_chunks(q, C), to_chunks(k, C), to_chunks(v, C)
    gc = jnp.cumsum(to_chunks(g, C), axis=-1)
    bc = to_chunks(beta, C)
    tri_incl = np.tril(np.ones((C, C), dtype=bool))
    tri_strict = np.tril(np.ones((C, C), dtype=bool), -1)
    decay = jnp.exp(jnp.where(tri_incl, gc[..., :, None] - gc[..., None, :], -jnp.inf))
    kk = jnp.einsum('bhnid,bhnjd->bhnij', kc, kc)
    low = jnp.where(tri_strict, bc[..., :, None] * kk * decay, 0.0)
    rhs = jnp.concatenate([vc * bc[..., None], kc * (bc * jnp.exp(gc))[..., None]], axis=-1)
    sol = lax.linalg.triangular_solve(low, rhs, left_side=True, lower=True, unit_diagonal=True)
    u, w = sol[..., :dv], sol[..., dv:]
    attn = jnp.einsum('bhnid,bhnjd->bhnij', qc, kc) * decay
    q_dec = qc * jnp.exp(gc)[..., None]
    g_last = gc[..., -1]
    k_dec = kc * jnp.exp(g_last[..., None] - gc)[..., None]

    def step(s, xs):
        u_n, w_n, a_n, qd_n, kd_n, gl_n = xs
        v_new = u_n - jnp.einsum('bhcd,bhde->bhce', w_n, s)
        o_n = jnp.einsum('bhcd,bhde->bhce', qd_n, s) + jnp.einsum('bhij,bhje->bhie', a_n, v_new)
        s = s * jnp.exp(gl_n)[..., None, None] + jnp.einsum('bhcd,bhce->bhde', kd_n, v_new)
        return s, o_n

    xs = tuple(jnp.moveaxis(a, 2, 0) for a in (u, w, attn, q_dec, k_dec, g_last))
    s_fin, o = lax.scan(step, s0, xs)
    o = jnp.moveaxis(o, 0, 2)
    return jnp.moveaxis(o, 1, 3).reshape(B, T, H, dv), s_fin


def gla_chunked(q, k, v, log_a, s0):
    B, T, H, dk = k.shape
    dv = v.shape[-1]
    C = GLA_CHUNK
    qc, kc, vc = to_chunks(q, C), to_chunks(k, C), to_chunks(v, C)
    b = jnp.cumsum(to_chunks(log_a, C), axis=3)
    tri = np.tril(np.ones((C, C), dtype=bool))[:, :, None]
    dec = jnp.exp(jnp.where(tri, b[..., :, None, :] - b[..., None, :, :], -jnp.inf))
    attn = jnp.einsum('bhnid,bhnjd,bhnijd->bhnij', qc, kc, dec)
    o = jnp.einsum('bhnij,bhnje->bhnie', attn, vc)
    b_last = b[..., -1, :]
    d_state = jnp.einsum('bhncd,bhnce->bhnde', kc * jnp.exp(b_last[..., None, :] - b), vc)

    def step(s, xs):
        ds_n, bl_n = xs
        return s * jnp.exp(bl_n)[..., None] + ds_n, s

    s_fin, s_start = lax.scan(step, s0, (jnp.moveaxis(d_state, 2, 0), jnp.moveaxis(b_last, 2, 0)))
    o = o + jnp.einsum('bhncd,bhnde->bhnce', qc * jnp.exp(b), jnp.moveaxis(s_start, 0, 2))
    return jnp.moveaxis(o, 1, 3).reshape(B, T, H, dv), s_fin


def trunk_layer(x, mod, lw, ctx):
    (w_in, q_g, k_g, rpb, dn_conv, a_log, dt_bias, dn_g, gla_w, gla_b, gla_g,
     w_br, w_o, n1, n2, w_up, f_conv, w_dn) = lw
    f32 = jnp.float32
    B, T, _ = x.shape
    sh1, sc1, gt1, sh2, sc2, gt2 = jnp.split(mod, 6, axis=-1)
    h = rms_norm(x, n1) * (1 + sc1) + sh1
    split_at = np.cumsum(PROJ_SIZES)[:-1].tolist()
    (na_q, na_k, na_v, dn_q, dn_k, dn_v, dn_gate, dn_beta, dn_a,
     gla_q, gla_k, gla_v, gla_gate, gla_lr, m_a, m_b, m_c) = jnp.split(h @ w_in, split_at, axis=-1)

    hs = (B, T, NA_HEADS, NA_HEAD_DIM)
    qa = rms_norm(na_q.reshape(hs), q_g) * (NA_HEAD_DIM ** -0.5)
    ka = rms_norm(na_k.reshape(hs), k_g)
    va = na_v.reshape(hs)
    if ctx is None:
        o_a = context_attention(qa, ka, va)
    else:
        o_a = neighbourhood_attention(qa, ka, va, ctx[0], ctx[1], rpb)
    o_a = o_a.reshape(B, T, NA_W)

    bs = (B, T, DN_HEADS, DN_HEAD_DIM)
    qkv = jax.nn.silu(dw_conv(jnp.concatenate([dn_q, dn_k, dn_v], axis=-1), dn_conv)).astype(f32)
    q_b, k_b, v_b = [part.reshape(bs) for part in jnp.split(qkv, 3, axis=-1)]
    q_b = l2_norm(q_b) * (DN_HEAD_DIM ** -0.5)
    k_b = l2_norm(k_b)
    beta = jax.nn.sigmoid(dn_beta.astype(f32)).reshape(B, T, 2, DN_HEADS)
    g = -jnp.exp(a_log.astype(f32)) * jax.nn.softplus(
        dn_a.astype(f32).reshape(B, T, 2, DN_HEADS) + dt_bias.astype(f32))
    if ctx is None:
        s0_dn = jnp.zeros((B, 2, DN_HEADS, DN_HEAD_DIM, DN_HEAD_DIM), f32)
    else:
        s0_dn = ctx[2].astype(f32)
    o_fw, s_fw = gated_delta_chunked(q_b, k_b, v_b, g[:, :, 0], beta[:, :, 0], s0_dn[:, 0])
    o_bw, s_bw = gated_delta_chunked(rev(q_b), rev(k_b), rev(v_b), rev(g[:, :, 1]), rev(beta[:, :, 1]), s0_dn[:, 1])
    o_b = rms_norm(o_fw + rev(o_bw), dn_g) * jax.nn.silu(dn_gate.astype(f32).reshape(bs))
    o_b = o_b.reshape(B, T, DN_W).astype(x.dtype)

    kshape = (B, T, GLA_HEADS, GLA_KEY_DIM)
    vshape = (B, T, GLA_HEADS, GLA_VAL_DIM)
    gl = jnp.einsum('btzr,zrk->btzk', gla_lr.reshape(B, T, 2, GLA_GATE_RANK), gla_w) + gla_b
    log_a = (jax.nn.log_sigmoid(gl.astype(f32)) / GLA_TAU).reshape(B, T, 2, GLA_HEADS, GLA_KEY_DIM)
    q_c = gla_q.astype(f32).reshape(kshape) * (GLA_KEY_DIM ** -0.5)
    k_c = gla_k.astype(f32).reshape(kshape)
    v_c = gla_v.astype(f32).reshape(vshape)
    if ctx is None:
        s0_gla = jnp.zeros((B, 2, GLA_HEADS, GLA_KEY_DIM, GLA_VAL_DIM), f32)
    else:
        s0_gla = ctx[3].astype(f32)
    oc_fw, sg_fw = gla_chunked(q_c, k_c, v_c, log_a[:, :, 0], s0_gla[:, 0])
    oc_bw, sg_bw = gla_chunked(rev(q_c), rev(k_c), rev(v_c), rev(log_a[:, :, 1]), s0_gla[:, 1])
    o_c = rms_norm(oc_fw + rev(oc_bw), gla_g) * jax.nn.silu(gla_gate.astype(f32).reshape(vshape))
    o_c = o_c.reshape(B, T, GLA_VW).astype(x.dtype)

    merged = (jax.nn.sigmoid(m_a) * (o_a @ w_br[0])
              + jax.nn.sigmoid(m_b) * (o_b @ w_br[1])
              + jax.nn.sigmoid(m_c) * (o_c @ w_br[2]))
    x = x + gt1 * (merged @ w_o)

    h2 = rms_norm(x, n2) * (1 + sc2) + sh2
    u_val, u_gate = jnp.split(dw_conv(h2 @ w_up, f_conv), 2, axis=-1)
    x = x + gt2 * ((jax.nn.silu(u_gate) * u_val) @ w_dn)

    if ctx is None:
        state = (ka, va,
                 jnp.stack([s_fw, s_bw], axis=1).astype(x.dtype),
                 jnp.stack([sg_fw, sg_bw], axis=1).astype(x.dtype))
    else:
        state = None
    return x, state


def setup_inputs(seed: int = 0) -> dict:
    key = jax.random.key(seed)
    ks = jax.random.split(key, 32)
    f32 = jnp.float32
    d = D_MODEL

    def nrm(k, shape, scale):
        return jax.random.normal(k, shape, f32) * scale

    dt = jnp.exp(jax.random.uniform(ks[15], (DEPTH, 2, DN_HEADS), f32, math.log(1e-3), math.log(0.1)))
    return {
        'x_prompt': nrm(ks[0], (BATCH, SEQ, d), 1.0),
        'x_sample': nrm(ks[1], (DEC_BATCH, DEC_SEQ, d), 1.0),
        'cache_k': nrm(ks[2], (DEC_BATCH, DEPTH, PAST_LEN, NA_HEADS, NA_HEAD_DIM), 1.0),
        'cache_v': nrm(ks[3], (DEC_BATCH, DEPTH, PAST_LEN, NA_HEADS, NA_HEAD_DIM), 1.0),
        'state_dn': nrm(ks[4], (DEC_BATCH, DEPTH, 2, DN_HEADS, DN_HEAD_DIM, DN_HEAD_DIM), 0.1),
        'state_gla': nrm(ks[5], (DEC_BATCH, DEPTH, 2, GLA_HEADS, GLA_KEY_DIM, GLA_VAL_DIM), 0.5),
        'c': nrm(ks[6], (DEC_BATCH, d), 1.0),
        'c_ctx': nrm(ks[7], (d,), 1.0),
        'w_ada': nrm(ks[8], (DEPTH, d, 6 * d), 0.5 * d ** -0.5),
        'b_ada': nrm(ks[9], (DEPTH, 6 * d), 0.02),
        'norm1': 1.0 + nrm(ks[10], (DEPTH, d), 0.05),
        'w_in': nrm(ks[11], (DEPTH, d, PROJ_DIM), d ** -0.5),
        'na_q_norm': 1.0 + nrm(ks[12], (DEPTH, NA_HEAD_DIM), 0.05),
        'na_k_norm': 1.0 + nrm(ks[13], (DEPTH, NA_HEAD_DIM), 0.05),
        'na_rpb': nrm(ks[14], (DEPTH, NA_HEADS, 2 * NA_WIN_H - 1, 2 * NA_WIN_W - 1), 0.1),
        'dn_conv': nrm(ks[16], (DEPTH, DN_CONV, 3 * DN_W), DN_CONV ** -0.5),
        'dn_a_log': jnp.log(jax.random.uniform(ks[17], (DEPTH, 2, DN_HEADS), f32, 1.0, 16.0)),
        'dn_dt_bias': dt + jnp.log(-jnp.expm1(-dt)),
        'dn_out_norm': 1.0 + nrm(ks[18], (DEPTH, DN_HEAD_DIM), 0.05),
        'gla_w_gate': nrm(ks[19], (DEPTH, 2, GLA_GATE_RANK, GLA_KW), GLA_GATE_RANK ** -0.5),
        'gla_b_gate': nrm(ks[20], (DEPTH, 2, GLA_KW), 0.1),
        'gla_out_norm': 1.0 + nrm(ks[21], (DEPTH, GLA_VAL_DIM), 0.05),
        'w_branch': nrm(ks[22], (DEPTH, N_BRANCH, BRANCH_W, d), BRANCH_W ** -0.5),
        'w_out': nrm(ks[23], (DEPTH, d, d), d ** -0.5),
        'norm2': 1.0 + nrm(ks[24], (DEPTH, d), 0.05),
        'w_up': nrm(ks[25], (DEPTH, d, 2 * D_FF), d ** -0.5),
        'ffn_conv': nrm(ks[26], (DEPTH, FFN_CONV, 2 * D_FF), FFN_CONV ** -0.5),
        'w_down': nrm(ks[27], (DEPTH, D_FF, d), D_FF ** -0.5),
    }


def reference(x_prompt, x_sample, cache_k, cache_v, state_dn, state_gla, c, c_ctx,
              w_ada, b_ada, norm1, w_in, na_q_norm, na_k_norm, na_rpb,
              dn_conv, dn_a_log, dn_dt_bias, dn_out_norm,
              gla_w_gate, gla_b_gate, gla_out_norm,
              w_branch, w_out, norm2, w_up, ffn_conv, w_down):
    y_p, y_s = x_prompt, x_sample
    new_k, new_v, new_dn, new_gla = [], [], [], []
    for l in range(DEPTH):
        lw = (w_in[l], na_q_norm[l], na_k_norm[l], na_rpb[l], dn_conv[l], dn_a_log[l], dn_dt_bias[l],
              dn_out_norm[l], gla_w_gate[l], gla_b_gate[l], gla_out_norm[l], w_branch[l], w_out[l],
              norm1[l], norm2[l], w_up[l], ffn_conv[l], w_down[l])
        mod_ctx = adaln(c_ctx[None, None, :], w_ada[l], b_ada[l]).astype(y_p.dtype)
        mod_lat = adaln(c[:, None, :], w_ada[l], b_ada[l]).astype(y_s.dtype)
        y_p, (k_l, v_l, dn_l, gla_l) = trunk_layer(y_p, mod_ctx, lw, None)
        y_s, _ = trunk_layer(y_s, mod_lat, lw,
                             (cache_k[:, l], cache_v[:, l], state_dn[:, l], state_gla[:, l]))
        new_k.append(k_l)
        new_v.append(v_l)
        new_dn.append(dn_l)
        new_gla.append(gla_l)
    return (y_p, y_s, jnp.stack(new_k, axis=1), jnp.stack(new_v, axis=1),
            jnp.stack(new_dn, axis=1), jnp.stack(new_gla, axis=1))
```

```python
import os
from contextlib import ExitStack
import numpy as np
import concourse.bass as bass
import concourse.mybir as mybir
from concourse.bass_utils import run_bass_kernel_spmd

F32 = mybir.dt.float32
AF = mybir.ActivationFunctionType
ALU = mybir.AluOpType

EPS = 1e-6
D = 1024
NP_TOK = 1024
NS_TOK = 2048
NTOK = NP_TOK + NS_TOK
PROJ = 8240
C_NAQ, C_NAK, C_NAV = 0, 512, 1024
C_DNQ, C_DNK, C_DNV, C_DNG = 1536, 2048, 2560, 3072
C_DNB = 3584
C_GQ, C_GK, C_GV, C_GG, C_GLR = 3600, 3856, 4112, 4624, 5136
C_M = (5168, 6192, 7216)
NEG = -30000.0
R_N1, R_N2, R_BADA, R_DNCONV, R_FCONV, R_QG, R_KG, R_DNG, R_GLAG, R_GLAB = 0, 8, 16, 64, 100, 232, 233, 234, 235, 236
K_ID, K_ONES, K_BLK, K_TU, K_TL, K_SU, K_SL = [i * 128 for i in range(7)]
NDMA_SEMS = 20
STAGE = int(os.environ.get("KSTAGE", "99"))
BF = bool(int(os.environ.get("KBF", "0")))
BF16 = mybir.dt.bfloat16
KDBG = int(os.environ.get("KDBG", "0"))
MMDT = BF16 if BF else F32
SWCAST = BF and bool(int(os.environ.get("KSWCAST", "1")))
QW = 'pool' if SWCAST else 'sp'
QA = 'sp' if SWCAST else 'pool'
PE2 = 'dve' if SWCAST else 'pool'


class K:
    def __init__(self, nc):
        self.nc = nc
        self.eng = {'pe': nc.tensor, 'act': nc.scalar, 'dve': nc.vector, 'pool': nc.gpsimd, 'sp': nc.sync}
        self.sem = {e: nc.alloc_semaphore('s_' + e) for e in ('pe', 'act', 'dve', 'pool')}
        self.cnt = {e: 0 for e in self.sem}
        self.waited = {e: {} for e in self.eng}
        self.res = {}
        self.children = {}
        self.dq = {}
        for q in ('sp', 'pool'):
            self.dq[q] = dict(sems=[nc.alloc_semaphore('d_%s%d' % (q, i)) for i in range(NDMA_SEMS)], n=0)
        self.n_inst = 0
        self.uid = 0

    def _wait(self, e, tok):
        if tok is None:
            return
        if tok[0] == 'e':
            src, val, sem = ('e', tok[1]), tok[2], self.sem[tok[1]]
        else:
            src, val, sem = ('d', tok[1], tok[2]), tok[3], self.dq[tok[1]]['sems'][tok[2]]
        if self.waited[e].get(src, 0) >= val:
            return
        self.eng[e].wait_ge(sem, val)
        self.waited[e][src] = val

    def _related(self, key):
        out = []
        for i in range(1, len(key) + 1):
            p = key[:i]
            if p in self.res:
                out.append(p)
        for c in self.children.get(key, ()):
            if c != key:
                out.append(c)
        return out

    def _get(self, key):
        if key not in self.res:
            self.res[key] = dict(w=None, r=[])
            for i in range(1, len(key) + 1):
                self.children.setdefault(key[:i], set()).add(key)
        return self.res[key]

    def _deps(self, e, r, w, acc=False):
        for key in r:
            for kk in self._related(key):
                self._wait(e, self.res[kk]['w'])
        for key in w:
            for kk in self._related(key):
                ent = self.res[kk]
                if ent['r']:
                    for t in ent['r']:
                        self._wait(e, t)
                elif not (acc and e == 'pe' and ent['w'] is not None and ent['w'][:2] == ('e', 'pe')):
                    self._wait(e, ent['w'])

    def _record(self, tok, r, w):
        src = tok[:2] if tok[0] == 'e' else tok[:3]
        for key in r:
            ent = self._get(key)
            ent['r'] = [t for t in ent['r'] if (t[:2] if t[0] == 'e' else t[:3]) != src]
            ent['r'].append(tok)
        for key in w:
            self._get(key)
            for kk in self._related(key) + [key]:
                self.res[kk]['w'] = tok
                self.res[kk]['r'] = []

    @staticmethod
    def _keys(ks):
        return [k if isinstance(k, tuple) else (k,) for k in ks]

    def op(self, e, fn, r=(), w=(), acc=False):
        r, w = self._keys(r), self._keys(w)
        self._deps(e, r, w, acc=acc)
        inst = fn(self.eng[e])
        self.cnt[e] += 1
        inst.then_inc(self.sem[e], 1)
        self._record(('e', e, self.cnt[e]), r, w)
        self.n_inst += 1
        return inst

    def dma(self, q, out, in_, r=(), w=(), **kw):
        r, w = self._keys(r), self._keys(w)
        d = self.dq[q]
        n = d['n']
        slot = n % NDMA_SEMS
        val = 16 * (n // NDMA_SEMS + 1)
        if n >= NDMA_SEMS:
            self._wait(q, ('d', q, slot, val - 16))
        self._deps(q, r, w)
        inst = self.eng[q].dma_start(out=out, in_=in_, **kw)
        inst.then_inc(d['sems'][slot], 16)
        d['n'] = n + 1
        self._record(('d', q, slot, val), r, w)
        self.n_inst += 1
        return inst

    def barrier(self, keep=()):
        toks = [('e', e, self.cnt[e]) for e in self.cnt if self.cnt[e] > 0]
        for q, d in self.dq.items():
            n = d['n']
            for slot in range(min(n, NDMA_SEMS)):
                last = ((n - 1 - slot) // NDMA_SEMS) * NDMA_SEMS + slot
                toks.append(('d', q, slot, 16 * (last // NDMA_SEMS + 1)))
        for e in ('pe', 'act', 'dve', 'pool', 'sp'):
            for t in toks:
                self._wait(e, t)
        self.res = {}
        self.children = {}


class Scope:
    def __init__(self, k):
        self.k = k
        self.es = ExitStack()

    def sb(self, name, shape, dt=F32):
        self.k.uid += 1
        return self.es.enter_context(self.k.nc.sbuf_tensor("%s_%d" % (name, self.k.uid), list(shape), dt)).ap()

    def close(self):
        self.k.barrier()
        self.es.close()


class Ring:
    def __init__(self, tiles, name, keys=None):
        self.t, self.name, self.i, self.keys = tiles, name, 0, keys

    def next(self):
        i = self.i % len(self.t)
        self.i += 1
        return self.t[i], (self.keys[i] if self.keys else (self.name, i))


def run_rr(gens):
    gens = list(gens)
    while gens:
        nxt = []
        for g in gens:
            try:
                next(g)
                nxt.append(g)
            except StopIteration:
                pass
        gens = nxt


def build_program():
    nc = bass.Bass("TRN2", target_bir_lowering=False)
    k = K(nc)

    def din(name, shape):
        return nc.dram_tensor(name, list(shape), F32, kind="ExternalInput").ap()

    def dout(name, shape):
        return nc.dram_tensor(name, list(shape), F32, kind="ExternalOutput").ap()

    xp = din("xp", [NP_TOK, D]); xs = din("xs", [NS_TOK, D])
    ck = din("ck", [2, 512, 512]); cv = din("cv", [2, 512, 512])
    sdn = din("sdn", [2, 2, 4, 128, 128]); sgla = din("sgla", [2, 2, 4, 64, 128])
    cvec = din("cvec", [16, 128])
    w_ada = din("w_ada", [2, D, 6144]); b_ada = din("b_ada", [2, 6144])
    w_in = din("w_in", [2, D, PROJ]); w_br = din("w_br", [2, 3, 512, D]); w_out = din("w_out", [2, D, D])
    w_up = din("w_up", [2, D, 5632]); w_dn = din("w_dn", [2, 2816, D])
    pf = din("pf", [2, 256, 128]); rtab = din("rtab", [2, 8, 128, 1408])
    dnab = din("dnab", [2, 2, 8]); glaw = din("glaw", [2, 2, 16, 256]); consts_d = din("consts", [128, 896])
    selc_d = din("selc", [128, 16])
    yp = dout("yp", [NP_TOK, D]); ys = dout("ys", [NS_TOK, D]); ys_own = dout("ys_own", [512, D])
    nk = dout("nk", [4, 2, 256, 512]); nv = dout("nv", [4, 2, 256, 512])
    ndn = dout("ndn", [4, 2, 2, 4, 128, 128]); ngla = dout("ngla", [4, 2, 2, 4, 64, 128])
    OSC = [nc.dram_tensor("osc%d" % i, [512, NTOK], MMDT, kind="Internal").ap() for i in range(3)]

    PS = [nc.alloc_psum_tensor("psb%d" % i, [128, 512], F32).ap() for i in range(8)]
    psr = Ring(PS[0:4], 'ps')
    PK = [('ps', i) for i in range(4)] + [('psx', i) for i in range(4, 8)]
    psr8 = Ring(PS, 'ps8', keys=PK)

    top = Scope(k)
    CONST = top.sb("const", [128, 896])
    pT = [top.sb("pT%d" % l, [128, 256]) for l in range(2)]
    scT = top.sb("scT", [128, 16])
    k.dma('sp', CONST, consts_d, w=['const'])
    selc = top.sb("selc", [128, 16])
    k.dma('sp', selc, selc_d, w=['selc'])
    ident = CONST[:, K_ID:K_ID + 128]; ones = CONST[:, K_ONES:K_ONES + 128]; blk = CONST[:, K_BLK:K_BLK + 128]
    UCUM = (CONST[:, K_TU:K_TU + 128], CONST[:, K_TL:K_TL + 128])
    MINC = UCUM
    MSTR = (CONST[:, K_SL:K_SL + 128], CONST[:, K_SU:K_SU + 128])

    def act(out, in_, func, r, w, bias=None, scale=None, accum=None):
        kw = {}
        if bias is not None: kw['bias'] = bias
        if scale is not None: kw['scale'] = scale
        if accum is not None: kw['accum_out'] = accum
        return k.op('act', lambda e: e.activation(out=out, in_=in_, func=func, **kw), r=r, w=w)

    def mm(out, lhsT, rhs, r, w, start=True, stop=True):
        return k.op('pe', lambda e: e.matmul(out, lhsT=lhsT, rhs=rhs, start=start, stop=stop), r=r, w=w, acc=not start)

    def tr(out, in_, idn, r, w):
        return k.op('pe', lambda e: e.transpose(out, in_, idn), r=r + ['const'], w=w)

    def tt(out, in0, in1, op, r, w, e='dve'):
        return k.op(e, lambda en: en.tensor_tensor(out=out, in0=in0, in1=in1, op=op), r=r, w=w)

    def ts(out, in0, s1, op0, r, w, s2=None, op1=None, e='dve'):
        if op1 is None:
            return k.op(e, lambda en: en.tensor_scalar(out=out, in0=in0, scalar1=s1, scalar2=None, op0=op0), r=r, w=w)
        return k.op(e, lambda en: en.tensor_scalar(out=out, in0=in0, scalar1=s1, scalar2=s2, op0=op0, op1=op1), r=r, w=w)

    def stt(out, in0, sc, in1, op0, op1, r, w):
        return k.op('dve', lambda en: en.scalar_tensor_tensor(out=out, in0=in0, scalar=sc, in1=in1, op0=op0, op1=op1), r=r, w=w)

    def recip(out, in_, r, w):
        return k.op('dve', lambda en: en.reciprocal(out=out, in_=in_), r=r, w=w)

    def cp(out, in_, r, w, e='dve'):
        return k.op(e, lambda en: en.tensor_copy(out=out, in_=in_), r=r, w=w)

    def mset(ap, val, w, e=PE2):
        return k.op(e, lambda en: en.memset(ap, val), w=w)

    s0 = Scope(k)
    pfs = s0.sb("pfs", [128, 2, 2, 128])
    cvs = s0.sb("cvs", [16, 128])
    k.dma('sp', pfs, pf.rearrange("l (a p) c -> p l a c", p=128), w=['pfs'])
    k.dma('sp', cvs, cvec, w=['cvs'])
    for l in range(2):
        for a in range(2):
            ps, pk = psr.next()
            tr(ps[:, 0:128], pfs[:, l, a, :], ident, r=['pfs'], w=[pk])
            cp(pT[l][:, a * 128:(a + 1) * 128], ps[:, 0:128], r=[pk], w=['pT'])
    ps, pk = psr.next()
    tr(ps[:, 0:16], cvs[0:16, :], ident[0:16, 0:16], r=['cvs'], w=[pk])
    act(scT, ps[:, 0:16], AF.Silu, r=[pk], w=['scT'])
    s0.close()

    GP = dict(name='P', t0=0, N=NP_TOK, seqs=[(i * 256, 256) for i in range(4)], cond=0, xin=xp, y=yp)
    GS = dict(name='S', t0=NP_TOK, N=NS_TOK, seqs=[(0, 2048)], cond=1, xin=xs, y=ys)

    wstage = Ring([top.sb("wst_%d" % i, [128, 8, 128]) for i in range(3)], 'wst') if (BF and not SWCAST) else None
    cast_i = [0]

    def load_w(dst, dkey, src3, a, b):
        if not BF or SWCAST:
            k.dma(QW, dst, src3, w=[dkey])
            return
        st, stk = wstage.next()
        k.dma(QW, st[:, 0:a, 0:b], src3, w=[stk])
        cast_i[0] += 1
        if cast_i[0] % 2:
            cp(dst, st[:, 0:a, 0:b], r=[stk], w=[dkey])
        else:
            act(dst, st[:, 0:a, 0:b], AF.Copy, r=[stk], w=[dkey])

    def wtile_load(ring, src, M, nk=8, q=QW, cast=True):
        wt, wk = ring.next()
        src3 = src.rearrange("(kc p) m -> p kc m", p=128)
        if cast:
            load_w(wt[:, 0:nk, 0:M], wk, src3, nk, M)
        else:
            k.dma(q, wt[:, 0:nk, 0:M], src3, w=[wk])
        return wt, wk

    for l in range(2):
        lay = Scope(k)
        modF = lay.sb("modF", [128, 4, 8, 2])
        a_sc = lay.sb("a_sc", [128, 2, 8, 2])
        gtb = lay.sb("gtb", [128, 2, 2, D])
        qgs = lay.sb("qgs", [128, 1])
        ngb = lay.sb("ngb", [128, 4])
        dtb = lay.sb("dtb", [128, 8]); nexpA = lay.sb("nexpA", [128, 8])
        glw = lay.sb("glw", [16, 2, 256])
        P = pT[l]
        ph = Scope(k)
        wring = Ring([ph.sb("w0_%d" % i, [128, 8, 128]) for i in range(3)], 'w0')
        bb = ph.sb("bb", [128, 2, D])
        scB = ph.sb("scB", [128, 16, 128])
        for i in range(16):
            cp(scB[:, i, :], scT[:, i:i + 1].to_broadcast([128, 128]), r=['scT'], w=['scB'])
        for gi, col0 in enumerate((2 * D, 5 * D)):
            k.dma('sp', bb[:, gi, :], b_ada[l:l + 1, col0:col0 + D].partition_broadcast(128), w=[('bb', gi)])
        for kind, fc0 in enumerate((0, 8, 24, 32)):
            for j in range(8):
                fc = fc0 + j
                wt, wk = wtile_load(wring, w_ada[l, :, fc * 128:(fc + 1) * 128], 128, q='sp', cast=False)
                ps, pk = psr.next()
                for kc in range(8):
                    mm(ps[:, 0:2], wt[:, kc, :], scT[:, kc::8], r=[wk, 'scT'], w=[pk], start=kc == 0, stop=kc == 7)
                act(modF[:, kind, j, :], ps[:, 0:2], AF.Identity, r=[pk, 'pT'], w=['modF'],
                    bias=P[:, R_BADA + fc:R_BADA + fc + 1])
        for which, (kind, rn) in enumerate(((1, R_N1), (3, R_N2))):
            for j in range(8):
                ts(a_sc[:, which, j, :], modF[:, kind, j, :], 1.0, ALU.add, r=['modF', 'pT'], w=['a_sc'],
                   s2=P[:, rn + j:rn + j + 1], op1=ALU.mult)
        for gi, col0 in enumerate((2 * D, 5 * D)):
            for j in range(8):
                wt, wk = wtile_load(wring, w_ada[l, :, col0 + j * 128:col0 + (j + 1) * 128], 128, q='sp', cast=False)
                for cond in range(2):
                    ps, pk = psr.next()
                    for kc in range(8):
                        mm(ps[:, 0:128], scB[:, cond * 8 + kc, :], wt[:, kc, :], r=[wk, 'scB'], w=[pk],
                           start=kc == 0, stop=kc == 7)
                    tt(gtb[:, gi, cond, j * 128:(j + 1) * 128], ps[:, 0:128], bb[:, gi, j * 128:(j + 1) * 128], ALU.add,
                       r=[pk, ('bb', gi)], w=['gtb'])
        ts(qgs, P[:, R_QG:R_QG + 1], 0.125, ALU.mult, r=['pT'], w=['qgs'])
        ts(ngb, P[:, R_GLAB:R_GLAB + 4], -1.0, ALU.mult, r=['pT'], w=['ngb'])
        k.dma('sp', dtb, dnab[l, 1:2, :].partition_broadcast(128), w=['dtb'])
        k.dma('sp', nexpA, dnab[l, 0:1, :].partition_broadcast(128), w=['nexpA'])
        act(nexpA, nexpA, AF.Exp, r=['nexpA'], w=['nexpA'])
        ts(nexpA, nexpA, -1.0, ALU.mult, r=['nexpA'], w=['nexpA'])
        k.dma('sp', glw, glaw[l].rearrange("z r c -> r z c"), w=['glw'])
        ph.close()

        def norm_to_hT(sc, G, xsrc, xkeyf, hT, which, ncols_extra=None):
            pass

        for G in (GP, GS):
            N, cond, t0 = G['N'], G['cond'], G['t0']
            nch = N // 128
            xsrc = G['xin'] if l == 0 else G['y']
            gs = Scope(k)
            hT = gs.sb("hT", [128, 8, N], MMDT)
            p1 = Scope(k)
            xring = Ring([p1.sb("x1_%d" % i, [128, D]) for i in range(3)], 'x1')
            xnring = Ring([p1.sb("xn_%d" % i, [128, D]) for i in range(2)], 'xn')
            junk = p1.sb("junk", [128, D]); ssq = p1.sb("ssq", [128, 4])
            for s in range(nch):
                xt, xk = xring.next()
                k.dma(QA, xt, xsrc[s * 128:(s + 1) * 128, :], r=[('y', G['name'], s)], w=[xk])
                sq = ssq[:, (s % 4):(s % 4) + 1]; sqk = ('ssq', s % 4)
                act(junk, xt, AF.Square, r=[xk], w=['junk', sqk], accum=sq)
                act(sq, sq, AF.Sqrt, r=[sqk], w=[sqk], scale=1.0 / D, bias=EPS)
                recip(sq, sq, r=[sqk], w=[sqk])
                xn, xnk = xnring.next()
                ts(xn, xt, sq, ALU.mult, r=[xk, sqk], w=[xnk])
                for half in range(2):
                    ps, pk = psr.next()
                    for j in range(4):
                        kc = half * 4 + j
                        tr(ps[:, j * 128:(j + 1) * 128], xn[:, kc * 128:(kc + 1) * 128], ident, r=[xnk], w=[pk])
                    for j in range(4):
                        kc = half * 4 + j
                        act(hT[:, kc, s * 128:(s + 1) * 128], ps[:, j * 128:(j + 1) * 128], AF.Identity,
                            r=[pk, 'a_sc', 'modF'], w=[('hT', kc, s)],
                            scale=a_sc[:, 0, kc, cond:cond + 1], bias=modF[:, 0, kc, cond:cond + 1])
            p1.close()

            def proj_fm(wring, col0, M, evac, nk_=8):
                wt, wk = wtile_load(wring, w_in[l, :, col0:col0 + M], M)
                for n0 in range(0, N, 512):
                    ps, pk = psr.next()
                    for kc in range(8):
                        mm(ps[0:M, :], wt[:, kc, 0:M], hT[:, kc, n0:n0 + 512], r=[wk, ('hT', kc)], w=[pk],
                           start=kc == 0, stop=kc == 7)
                    evac(ps, pk, n0)

            def proj_tm(wring, col0, M, evac):
                wt, wk = wtile_load(wring, w_in[l, :, col0:col0 + M], M)
                for s in range(nch):
                    ps, pk = psr.next()
                    for kc in range(8):
                        mm(ps[:, 0:M], hT[:, kc, s * 128:(s + 1) * 128], wt[:, kc, 0:M], r=[wk, ('hT', kc, s)], w=[pk],
                           start=kc == 0, stop=kc == 7)
                    evac(ps, pk, s)

            if STAGE >= 1 and not ((KDBG & 4) and G is GS) and not ((KDBG & 8) and G is GP):
                na = Scope(k)
                wring = Ring([na.sb("wA_%d" % i, [128, 8, 128], MMDT) for i in range(2)], 'wA')
                qT = na.sb("qT", [128, N], MMDT); kT = na.sb("kT", [128, N]); vtm = na.sb("vtm", [128, nch, 128])
                oT = na.sb("oT", [128, N], MMDT)
                vaug = [na.sb("vaug%d" % h_, [128, nch, 128], MMDT) for h_ in range(2)]
                for h_ in range(2):
                    mset(vaug[h_], 1.0, w=[('vaug', h_)])
                if BF:
                    kTb = na.sb("kTb", [128, N], MMDT)
                    ebring = Ring([na.sb("Eb_%d" % i, [128, 512]) for i in range(2)], 'Eb')
                else:
                    kTb = kT
                raw = na.sb("raw", [128, 512]); sqb = na.sb("sqb", [128, 512]); rsb = na.sb("rsb", [128, 512])
                ering = Ring([na.sb("E_%d" % i, [128, 512], MMDT) for i in range(5)], 'E')
                rd = na.sb("rd", [128, 512])
                ktm = Ring([na.sb("ktm_%d" % i, [128, 128]) for i in range(2)], 'ktm')
                if G is GS:
                    kcx = na.sb("kcx", [128, 512], MMDT); kcl = na.sb("kcl", [128, 4, 128]); vcx = na.sb("vcx", [128, 4, 128], MMDT)
                    Rt = na.sb("Rt", [128, 2, 1408])
                    vcxa = [na.sb("vcxa%d" % h_, [128, 4, 128], MMDT) for h_ in range(2)]
                    for h_ in range(2):
                        mset(vcxa[h_], 1.0, w=[('vcxa', h_)])
                for c in range(4):
                    def evac_qk(dst, dkey, gcol):
                        def f(ps, pk, n0):
                            act(raw, ps, AF.Copy, r=[pk], w=['raw'])
                            act(sqb, ps, AF.Square, r=[pk], w=['sqb'])
                            p2, pk2 = psr.next()
                            mm(p2, blk, sqb, r=['const', 'sqb'], w=[pk2])
                            act(rsb, p2, AF.Sqrt, r=[pk2], w=['rsb'], scale=1.0 / 64, bias=EPS)
                            recip(rsb, rsb, r=['rsb'], w=['rsb'])
                            stt(dst[:, n0:n0 + 512], raw, gcol, rsb, ALU.mult, ALU.mult, r=['raw', 'rsb', 'qgs', 'pT'],
                                w=[(dkey, n0 // 512)])
                            if BF and dkey == 'kT':
                                act(kTb[:, n0:n0 + 512], dst[:, n0:n0 + 512], AF.Copy, r=[(dkey, n0 // 512)], w=[('kTb', n0 // 512)])
                        return f
                    if not (KDBG & 16):
                        proj_fm(wring, C_NAQ + c * 128, 128, evac_qk(qT, 'qT', qgs[:, 0:1]))
                        proj_fm(wring, C_NAK + c * 128, 128, evac_qk(kT, 'kT', P[:, R_KG:R_KG + 1]))

                    def evac_v(ps, pk, s):
                        act(vtm[:, s, :], ps[:, 0:128], AF.Copy, r=[pk], w=[('vtm', s)])
                        cp(vaug[0][:, s, 0:64], vtm[:, s, 0:64], r=[('vtm', s)], w=[('vaug', 0, s)])
                        cp(vaug[1][:, s, 64:128], vtm[:, s, 64:128], r=[('vtm', s)], w=[('vaug', 1, s)], e=PE2)
                    if not (KDBG & 32):
                        proj_tm(wring, C_NAV + c * 128, 128, evac_v)
                    if G is GP:
                        for si, (q0, T) in enumerate(G['seqs'] if not (KDBG & 64) else []):
                            for j in range(T // 128):
                                s = q0 // 128 + j
                                k.dma(QA, nv[si, l, j * 128:(j + 1) * 128, c * 128:(c + 1) * 128], vtm[:, s, :],
                                      r=[('vtm', s)])
                                ps, pk = psr.next()
                                tr(ps[:, 0:128], kT[:, s * 128:(s + 1) * 128], ident, r=[('kT', s // 4)], w=[pk])
                                kt_, ktk = ktm.next()
                                cp(kt_, ps[:, 0:128], r=[pk], w=[ktk])
                                k.dma(QA, nk[si, l, j * 128:(j + 1) * 128, c * 128:(c + 1) * 128], kt_, r=[ktk])
                        for si, (q0, T) in enumerate(G['seqs'] if not (KDBG & 1) else []):
                            for hh in range(2):
                                pr = slice(64 * hh, 64 * hh + 64)
                                po = slice(64 * (1 - hh), 64 * (1 - hh) + 64)
                                O, Ok = PS[4 + (si * 2 + hh) % 4], PK[4 + (si * 2 + hh) % 4]
                                for kc2 in range(2):
                                    s = q0 // 128 + kc2
                                    ps, pk = psr.next()
                                    mm(ps[:, 0:T], kTb[pr, s * 128:(s + 1) * 128], qT[pr, q0:q0 + T],
                                       r=[('kT', s // 4), ('kTb', s // 4), ('qT', q0 // 512)], w=[pk])
                                    E, Ek = ering.next()
                                    act(E[:, 0:T], ps[:, 0:T], AF.Exp, r=[pk], w=[Ek])
                                    mm(O[:, 0:T], vaug[hh][:, s, :], E[:, 0:T], r=[('vaug', hh, s), Ek], w=[Ok], start=kc2 == 0, stop=kc2 == 1)
                                recip(rd[po, 0:T], O[po, 0:T], r=[Ok], w=['rd'])
                                tt(oT[pr, q0:q0 + T], O[pr, 0:T], rd[po, 0:T], ALU.mult, r=[Ok, 'rd'], w=[('oT', si, hh)])
                    else:
                        k.dma(QA, kcl, ck[l, :, c * 128:(c + 1) * 128].rearrange("(m p) f -> p m f", p=128), w=['kcl'])
                        load_w(vcx, 'vcx', cv[l, :, c * 128:(c + 1) * 128].rearrange("(m p) f -> p m f", p=128), 4, 128)
                        cp(vcxa[0][:, :, 0:64], vcx[:, :, 0:64], r=['vcx'], w=[('vcxa', 0)])
                        cp(vcxa[1][:, :, 64:128], vcx[:, :, 64:128], r=['vcx'], w=[('vcxa', 1)], e=PE2)
                        k.dma('sp', Rt, rtab[l, 2 * c:2 * c + 2].rearrange("h p f -> p h f"), w=['Rt'])
                        for m in range(4):
                            ps, pk = psr.next()
                            tr(ps[:, 0:128], kcl[:, m, :], ident, r=['kcl'], w=[pk])
                            cp(kcx[:, m * 128:(m + 1) * 128], ps[:, 0:128], r=[pk], w=['kcx'])
                        for hh in range(2 if not (KDBG & 2) else 0):
                            pr = slice(64 * hh, 64 * hh + 64)
                            po = slice(64 * (1 - hh), 64 * (1 - hh) + 64)
                            for g in range(4):
                                O, Ok = PS[4 + g], PK[4 + g]
                                qs = slice(g * 512, (g + 1) * 512)
                                rows = [8 * g + b for b in range(8)]
                                starts = [min(max(r_ - 4, 0), 24) for r_ in rows]
                                kr_lo = (min(starts) // 2) * 2
                                kr_hi = max(starts) + 7
                                chunks = [('ctx', m) for m in range(4)] + [('loc', kr0) for kr0 in range(kr_lo, kr_hi + 1, 2)]
                                def pv(vl, vk, E, Ek, first, last, c_lo, c_hi, O=O, Ok=Ok):
                                    mm(O[:, c_lo:c_hi], vl, E[:, 0:c_hi - c_lo], r=[vk, Ek], w=[Ok], start=first, stop=last)
                                pend = None
                                for ci, (kind, idx) in enumerate(chunks):
                                    first, last = ci == 0, ci == len(chunks) - 1
                                    ps, pk = psr.next()
                                    E, Ek = ering.next()
                                    if kind == 'loc':
                                        kr0 = idx
                                        s = kr0 // 2
                                        vbs = [[b for b in range(8) if starts[b] <= kr0 + a <= starts[b] + 7] for a in range(2)]
                                        allb = sorted(set(vbs[0] + vbs[1]))
                                        c_lo, c_hi = allb[0] * 64, (allb[-1] + 1) * 64
                                        W = c_hi - c_lo
                                        mm(ps[:, 0:W], kTb[pr, s * 128:(s + 1) * 128], qT[pr, g * 512 + c_lo:g * 512 + c_hi],
                                           r=[('kT', s // 4), ('kTb', s // 4), ('qT', g)], w=[pk])
                                        j0 = 10 - (kr0 - 8 * g)
                                        if BF:
                                            Eb, Ebk = ebring.next()
                                        else:
                                            Eb, Ebk = E, Ek
                                        tt(Eb[:, 0:W], ps[:, 0:W], Rt[:, hh, j0 * 64 + c_lo:j0 * 64 + c_hi], ALU.add, r=[pk, 'Rt'], w=[Ebk])
                                        for a in range(2):
                                            vb = vbs[a]
                                            pa = slice(64 * a, 64 * a + 64)
                                            if vb:
                                                lo, hi = vb[0] * 64 - c_lo, (vb[-1] + 1) * 64 - c_lo
                                                act(E[pa, lo:hi], Eb[pa, lo:hi], AF.Exp, r=[Ebk], w=[Ek])
                                                if lo > 0:
                                                    mset(E[pa, 0:lo], 0.0, w=[Ek])
                                                if hi < W:
                                                    mset(E[pa, hi:W], 0.0, w=[Ek])
                                            else:
                                                mset(E[pa, 0:W], 0.0, w=[Ek])
                                        vl, vk = vaug[hh][:, s, :], ('vaug', hh, s)
                                    else:
                                        m = idx
                                        mm(ps, kcx[pr, m * 128:(m + 1) * 128], qT[pr, qs], r=['kcx', ('qT', g)], w=[pk])
                                        act(E, ps, AF.Exp, r=[pk], w=[Ek])
                                        vl, vk = vcxa[hh][:, m, :], ('vcxa', hh)
                                        c_lo, c_hi = 0, 512
                                    if pend is not None:
                                        pv(*pend)
                                    pend = (vl, vk, E, Ek, first, last, c_lo, c_hi)
                                pv(*pend)
                                recip(rd[po, :], O[po, :], r=[Ok], w=['rd'])
                                tt(oT[pr, qs], O[pr, :], rd[po, :], ALU.mult, r=[Ok, 'rd'], w=[('oT', g, hh)])
                    if not (KDBG & 128):
                        k.dma(QA, OSC[0][c * 128:(c + 1) * 128, t0:t0 + N], oT, r=['oT'], w=[('osc', 0, G['name'], c)])
                na.close()

            if STAGE >= 2:
                dn = Scope(k)
                wring = Ring([dn.sb("wB_%d" % i, [128, 8, 128], MMDT) for i in range(2)], 'wB')
                Bt = dn.sb("Bt", [128, nch, 8]); NBt = dn.sb("NBt", [128, nch, 8]); Gt = dn.sb("Gt", [128, nch, 8])
                GC = dn.sb("GC", [128, nch, 8]); GL = dn.sb("GL", [128, nch, 8]); NGC = dn.sb("NGC", [128, nch, 8])
                EG = dn.sb("EG", [128, nch, 8]); EDL = dn.sb("EDL", [128, nch, 8]); EGL = dn.sb("EGL", [128, nch, 8])
                BEG = dn.sb("BEG", [128, nch, 8]); tmp8 = dn.sb("tmp8", [128, 8])

                def evac_bg(ps, pk, s):
                    act(Bt[:, s, :], ps[:, 0:8], AF.Sigmoid, r=[pk], w=['Bt', 'bgsync'])
                    tt(tmp8, ps[:, 8:16], dtb, ALU.add, r=[pk, 'dtb', 'bgsync'], w=['tmp8'])
                    act(tmp8, tmp8, AF.Exp, r=['tmp8'], w=['tmp8'])
                    act(tmp8, tmp8, AF.Ln, r=['tmp8'], w=['tmp8'], bias=1.0)
                    tt(Gt[:, s, :], tmp8, nexpA, ALU.mult, r=['tmp8', 'nexpA'], w=['Gt'])
                proj_tm(wring, C_DNB, 16, evac_bg)
                for s in range(nch):
                    for z in range(2):
                        ps, pk = psr.next()
                        mm(ps[:, 0:4], UCUM[z], Gt[:, s, 4 * z:4 * z + 4], r=['const', 'Gt'], w=[pk])
                        mm(ps[:, 4:8], ones, Gt[:, s, 4 * z:4 * z + 4], r=['const', 'Gt'], w=[pk])
                        cp(GC[:, s, 4 * z:4 * z + 4], ps[:, 0:4], r=[pk], w=['GC'])
                        cp(GL[:, s, 4 * z:4 * z + 4], ps[:, 4:8], r=[pk], w=['GL'])
                ts(NBt, Bt, -1.0, ALU.mult, r=['Bt'], w=['NBt'])
                ts(NGC, GC, -1.0, ALU.mult, r=['GC'], w=['NGC'])
                act(EG, GC, AF.Exp, r=['GC'], w=['EG'])
                act(EGL, GL, AF.Exp, r=['GL'], w=['EGL'])
                tt(EDL, GL, GC, ALU.subtract, r=['GL', 'GC'], w=['EDL'])
                act(EDL, EDL, AF.Exp, r=['EDL'], w=['EDL'])
                tt(BEG, Bt, EG, ALU.mult, r=['Bt', 'EG'], w=['BEG'])

                qTd = dn.sb("qTd", [128, N]); kTd = dn.sb("kTd", [128, N]); vTd = dn.sb("vTd", [128, N]); sgT = dn.sb("sgT", [128, N])
                ktm_ = dn.sb("ktmd", [128, nch, 128]); vtm_ = dn.sb("vtmd", [128, nch, 128]); oacc = dn.sb("oacc", [128, nch, 128])
                sq5 = dn.sb("sq5", [128, 512]); rs5 = dn.sb("rs5", [128, 512])
                Sst = [dn.sb("Sst%d" % z, [128, 128]) for z in range(2)]
                oTd = dn.sb("oTd", [128, N])
                raw2 = dn.sb("raw2", [128, N])
                rawbuf = [(oTd, 'oTd'), (raw2, 'raw2')]
                rawi = [0]
                oTb = dn.sb("oTb", [128, N], MMDT) if BF else oTd
                mring = {}
                for nm, cnt in (('Gb', 2), ('E1', 4), ('EmT', 2), ('EmS', 2), ('A', 4), ('Nn', 4), ('R', 4), ('attT', 4),
                                ('bv', 2), ('kbe', 2), ('kdec', 4), ('u', 4), ('wT', 4), ('vnew', 2), ('on', 2)):
                    mring[nm] = Ring([dn.sb("%s_%d" % (nm, i), [128, 128]) for i in range(cnt)], nm)
                ssd = dn.sb("ssd", [128, 4]); junkd = dn.sb("junkd", [128, 128])
                for hd in range(4):
                    def conv_silu(dst, dkey, jchunk):
                        rawd, rawk = rawbuf[rawi[0] % 2]

                        def f_all():
                            w0 = P[:, R_DNCONV + jchunk:R_DNCONV + jchunk + 1]
                            w1 = P[:, R_DNCONV + 12 + jchunk:R_DNCONV + 12 + jchunk + 1]
                            w2 = P[:, R_DNCONV + 24 + jchunk:R_DNCONV + 24 + jchunk + 1]
                            for (q0, T) in G['seqs']:
                                ts(dst[:, q0:q0 + T], rawd[:, q0:q0 + T], w1, ALU.mult, r=[rawk, 'pT'], w=[dkey])
                                stt(dst[:, q0 + 1:q0 + T], rawd[:, q0:q0 + T - 1], w0, dst[:, q0 + 1:q0 + T], ALU.mult, ALU.add,
                                    r=[rawk, 'pT', dkey], w=[dkey])
                                stt(dst[:, q0:q0 + T - 1], rawd[:, q0 + 1:q0 + T], w2, dst[:, q0:q0 + T - 1], ALU.mult, ALU.add,
                                    r=[rawk, 'pT', dkey], w=[dkey])
                            act(dst, dst, AF.Silu, r=[dkey], w=[dkey])
                        return f_all

                    def evac_raw(ps, pk, n0):
                        rawd, rawk = rawbuf[rawi[0] % 2]
                        act(rawd[:, n0:n0 + 512], ps, AF.Copy, r=[pk], w=[rawk])

                    def l2n(dst, dkey, const):
                        for n0 in range(0, N, 512):
                            act(sq5, dst[:, n0:n0 + 512], AF.Square, r=[dkey], w=['sq5'])
                            p2, pk2 = psr.next()
                            mm(p2, ones, sq5, r=['const', 'sq5'], w=[pk2])
                            act(rs5, p2, AF.Sqrt, r=[pk2], w=['rs5'], bias=EPS)
                            recip(rs5, rs5, r=['rs5'], w=['rs5'])
                            stt(dst[:, n0:n0 + 512], dst[:, n0:n0 + 512], const, rs5, ALU.mult, ALU.mult, r=[dkey, 'rs5'], w=[dkey])
                    proj_fm(wring, C_DNQ + hd * 128, 128, evac_raw); cq_ = conv_silu(qTd, 'qTd', hd); rawi[0] += 1
                    proj_fm(wring, C_DNK + hd * 128, 128, evac_raw); cq_(); l2n(qTd, 'qTd', 128 ** -0.5)
                    ck_ = conv_silu(kTd, 'kTd', 4 + hd); rawi[0] += 1
                    proj_fm(wring, C_DNV + hd * 128, 128, evac_raw); ck_(); l2n(kTd, 'kTd', 1.0)
                    conv_silu(vTd, 'vTd', 8 + hd)(); rawi[0] += 1

                    def evac_gate(ps, pk, n0):
                        act(sgT[:, n0:n0 + 512], ps, AF.Silu, r=[pk], w=['sgT'])
                    proj_fm(wring, C_DNG + hd * 128, 128, evac_gate)
                    for s in range(nch):
                        for (src, skey, dst, dkey) in ((kTd, 'kTd', ktm_, 'ktmd'), (vTd, 'vTd', vtm_, 'vtmd')):
                            ps, pk = psr.next()
                            tr(ps[:, 0:128], src[:, s * 128:(s + 1) * 128], ident, r=[skey], w=[pk])
                            act(dst[:, s, :], ps[:, 0:128], AF.Copy, r=[pk], w=[(dkey, s)])
                    mset(oacc, 0.0, w=['oacc'])

                    def unit_prep(z, s, U):
                        col = 4 * z + hd
                        c1 = slice(col, col + 1)
                        Gb, Gbk = mring['Gb'].next()
                        cp(Gb, Gt[:, s, c1].to_broadcast([128, 128]), r=['Gt'], w=[Gbk])
                        kc_ = kTd[:, s * 128:(s + 1) * 128]
                        ps1, pk1 = psr8.next()
                        mm(ps1[:, 0:128], Gb, UCUM[z], r=[Gbk, 'const'], w=[pk1])
                        ps2, pk2 = psr8.next()
                        mm(ps2[:, 0:128], kc_, kc_, r=['kTd'], w=[pk2])
                        yield
                        E1, E1k = mring['E1'].next()
                        act(E1, ps1[:, 0:128], AF.Exp, r=[pk1, 'NGC'], w=[E1k], bias=NGC[:, s, c1])
                        E2, E2k = mring['E1'].next()
                        act(E2, ps1[:, 0:128], AF.Exp, r=[pk1, 'GC'], w=[E2k], bias=GC[:, s, c1], scale=-1.0)
                        EmS, EmSk = mring['EmS'].next()
                        stt(EmS, E2, 1.0, MSTR[z], ALU.min, ALU.mult, r=[E2k, 'const'], w=[EmSk])
                        A0, Ak = mring['A'].next()
                        stt(A0, ps2[:, 0:128], NBt[:, s, c1], EmS, ALU.mult, ALU.mult, r=[pk2, 'NBt', EmSk], w=[Ak])
                        EmT, EmTk = mring['EmT'].next()
                        stt(EmT, E1, 1.0, MINC[z], ALU.min, ALU.mult, r=[E1k, 'const'], w=[EmTk])
                        ps3, pk3 = psr8.next()
                        tr(ps3[:, 0:128], A0, ident, r=[Ak], w=[pk3])
                        ps4, pk4 = psr8.next()
                        mm(ps4[:, 0:128], kc_, qTd[:, s * 128:(s + 1) * 128], r=['kTd', 'qTd'], w=[pk4])
                        yield
                        N0, Nk = mring['Nn'].next()
                        act(N0, ps3[:, 0:128], AF.Copy, r=[pk3], w=[Nk])
                        attT, attk = mring['attT'].next()
                        tt(attT, ps4[:, 0:128], EmT, ALU.mult, r=[pk4, EmTk], w=[attk])
                        Rm, Rk = mring['R'].next()
                        tt(Rm, N0, ident, ALU.add, r=[Nk, 'const'], w=[Rk], e=PE2)
                        Acur, Ack, Ncur, Nck = A0, Ak, N0, Nk
                        for lev in range(6):
                            pa, pak = psr8.next()
                            mm(pa[:, 0:128], Ncur, Acur, r=[Nck, Ack], w=[pak])
                            if lev < 5:
                                pn, pnk = psr8.next()
                                mm(pn[:, 0:128], Acur, Ncur, r=[Ack, Nck], w=[pnk])
                            yield
                            An, Ank = mring['A'].next()
                            act(An, pa[:, 0:128], AF.Copy, r=[pak], w=[Ank])
                            if lev < 5:
                                Nn, Nnk = mring['Nn'].next()
                                cp(Nn, pn[:, 0:128], r=[pnk], w=[Nnk])
                            prr, prk = psr8.next()
                            mm(prr[:, 0:128], An, Rm, r=[Ank, Rk], w=[prk])
                            yield
                            Rn, Rnk = mring['R'].next()
                            tt(Rn, prr[:, 0:128], Rm, ALU.add, r=[prk, Rk], w=[Rnk])
                            Rm, Rk = Rn, Rnk
                            Acur, Ack = An, Ank
                            if lev < 5:
                                Ncur, Nck = Nn, Nnk
                        bv, bvk = mring['bv'].next()
                        ts(bv, vtm_[:, s, :], Bt[:, s, c1], ALU.mult, r=[('vtmd', s), 'Bt'], w=[bvk])
                        kbe, kbek = mring['kbe'].next()
                        ts(kbe, ktm_[:, s, :], BEG[:, s, c1], ALU.mult, r=[('ktmd', s), 'BEG'], w=[kbek], e=PE2)
                        pu, puk = psr8.next()
                        mm(pu[:, 0:128], Rm, bv, r=[Rk, bvk], w=[puk])
                        pw, pwk = psr8.next()
                        mm(pw[:, 0:128], kbe, Rm, r=[kbek, Rk], w=[pwk])
                        yield
                        u, uk = mring['u'].next()
                        act(u, pu[:, 0:128], AF.Copy, r=[puk], w=[uk])
                        wT, wTk = mring['wT'].next()
                        act(wT, pw[:, 0:128], AF.Copy, r=[pwk], w=[wTk])
                        kdec, kdk = mring['kdec'].next()
                        ts(kdec, ktm_[:, s, :], EDL[:, s, c1], ALU.mult, r=[('ktmd', s), 'EDL'], w=[kdk], e=PE2)
                        U.update(u=(u, uk), wT=(wT, wTk), att=(attT, attk), kdec=(kdec, kdk))

                    def unit_step(z, s, U):
                        col = 4 * z + hd
                        c1 = slice(col, col + 1)
                        S_, Sk = Sst[z], ('Sst', z)
                        (u, uk), (wT, wTk), (attT, attk), (kdec, kdk) = U['u'], U['wT'], U['att'], U['kdec']
                        p1_, p1k = psr8.next()
                        mm(p1_[:, 0:128], wT, S_, r=[wTk, Sk], w=[p1k])
                        p2_, p2k = psr8.next()
                        mm(p2_[:, 0:128], qTd[:, s * 128:(s + 1) * 128], S_, r=['qTd', Sk], w=[p2k])
                        yield
                        vn, vnk = mring['vnew'].next()
                        tt(vn, u, p1_[:, 0:128], ALU.subtract, r=[uk, p1k], w=[vnk])
                        stt(oacc[:, s, :], p2_[:, 0:128], EG[:, s, c1], oacc[:, s, :], ALU.mult, ALU.add,
                            r=[p2k, 'EG', ('oacc', s)], w=[('oacc', s)])
                        p3_, p3k = psr8.next()
                        mm(p3_[:, 0:128], attT, vn, r=[attk, vnk], w=[p3k])
                        p4_, p4k = psr8.next()
                        mm(p4_[:, 0:128], kdec, vn, r=[kdk, vnk], w=[p4k])
                        yield
                        tt(oacc[:, s, :], p3_[:, 0:128], oacc[:, s, :], ALU.add, r=[p3k, ('oacc', s)], w=[('oacc', s)])
                        stt(S_, S_, EGL[:, s, c1], p4_[:, 0:128], ALU.mult, ALU.add, r=[Sk, 'EGL', p4k], w=[Sk])

                    for si, (q0, T) in enumerate(G['seqs']):
                        c0 = q0 // 128
                        ncs = T // 128
                        for z in range(2):
                            if G is GP:
                                mset(Sst[z], 0.0, w=[('Sst', z)])
                            else:
                                k.dma(QA, Sst[z], sdn[l, z, hd], w=[('Sst', z)])
                        Us = {}
                        for i in range(ncs + 1):
                            gens = []
                            if i < ncs:
                                Us[i] = ({}, {})
                                gens += [unit_prep(0, c0 + i, Us[i][0]), unit_prep(1, c0 + ncs - 1 - i, Us[i][1])]
                            if i >= 1:
                                gens += [unit_step(0, c0 + i - 1, Us[i - 1][0]), unit_step(1, c0 + ncs - i, Us[i - 1][1])]
                            run_rr(gens)
                        if G is GP:
                            for z in range(2):
                                k.dma(QA, ndn[si, l, z, hd], Sst[z], r=[('Sst', z)])
                    for s in range(nch):
                        sq = ssd[:, (s % 4):(s % 4) + 1]; sqk = ('ssd', s % 4)
                        act(junkd, oacc[:, s, :], AF.Square, r=[('oacc', s)], w=['junkd', sqk], accum=sq)
                        act(sq, sq, AF.Sqrt, r=[sqk], w=[sqk], scale=1.0 / 128, bias=EPS)
                        recip(sq, sq, r=[sqk], w=[sqk])
                        on, onk = mring['on'].next()
                        ts(on, oacc[:, s, :], sq, ALU.mult, r=[('oacc', s), sqk], w=[onk])
                        ps, pk = psr.next()
                        tr(ps[:, 0:128], on, ident, r=[onk], w=[pk])
                        stt(oTb[:, s * 128:(s + 1) * 128], ps[:, 0:128], P[:, R_DNG:R_DNG + 1], sgT[:, s * 128:(s + 1) * 128],
                            ALU.mult, ALU.mult, r=[pk, 'pT', 'sgT'], w=['oTd', 'oTb'])
                    k.dma(QA, OSC[1][hd * 128:(hd + 1) * 128, t0:t0 + N], oTb, r=['oTd', 'oTb'], w=[('osc', 1, G['name'], hd)])
                dn.close()

            if STAGE >= 3:
                gl = Scope(k)
                wring = Ring([gl.sb("wC_%d" % i, [128, 8, 128], MMDT) for i in range(2)], 'wC')
                wlr = gl.sb("wlr", [128, 8, 32], MMDT)
                lrt = gl.sb("lrt", [16, 512])
                qTg = gl.sb("qTg", [128, N]); kTg = gl.sb("kTg", [128, N])
                cs = gl.sb("cs", [128, N]); tmpN = gl.sb("tmpN", [128, N]); M0 = gl.sb("M0", [128, 512])
                qtl = gl.sb("qtl", [128, N]); ktl = gl.sb("ktl", [128, N]); kdT = gl.sb("kdT", [128, N])
                ebl = gl.sb("ebl", [128, nch]); tot = gl.sb("tot", [128, nch])
                vtg = [gl.sb("vtg%d" % h, [128, nch, 128]) for h in range(2)]
                sgg = gl.sb("sgg", [128, N])
                oag = [gl.sb("oag%d" % h, [128, nch, 128]) for h in range(2)]
                Sg = [gl.sb("Sg%d" % h, [128, 128]) for h in range(2)]
                oTg = gl.sb("oTgb", [128, N], MMDT) if BF else cs
                kdr = Ring([gl.sb("kd_%d" % i, [128, 128]) for i in range(2)], 'kd')
                atr = Ring([gl.sb("atg_%d" % i, [128, 128]) for i in range(3)], 'atg')
                onr = Ring([gl.sb("ong_%d" % i, [128, 128]) for i in range(2)], 'ong')
                ssg = gl.sb("ssg", [128, 4]); junkg = gl.sb("junkg", [128, 128])
                mset(M0, 1.0, w=['M0'])
                mset(M0.rearrange("p (c t) -> p c t", t=128)[:, :, 0:1], 0.0, w=['M0'])
                load_w(wlr, 'wlr', w_in[l, :, C_GLR:C_GLR + 32].rearrange("(kc p) m -> p kc m", p=128), 8, 32)
                for hh in range(2):
                    mset(Sg[hh], 0.0, w=[('Sg', hh)])
                for p2i in range(2):
                    def evq(ps, pk, n0):
                        act(qTg[:, n0:n0 + 512], ps, AF.Copy, r=[pk], w=['qTg'])

                    def evk(ps, pk, n0):
                        act(kTg[:, n0:n0 + 512], ps, AF.Copy, r=[pk], w=['kTg'])
                    proj_fm(wring, C_GQ + p2i * 128, 128, evq)
                    proj_fm(wring, C_GK + p2i * 128, 128, evk)
                    for hh in range(2):
                        h = 2 * p2i + hh

                        def evv(ps, pk, s, hh=hh):
                            act(vtg[hh][:, s, :], ps[:, 0:128], AF.Copy, r=[pk], w=[('vtg', hh, s)])
                        proj_tm(wring, C_GV + h * 128, 128, evv)
                    for z in range(2):
                        for n0 in range(0, N, 512):
                            pl, plk = psr.next()
                            for kc in range(8):
                                mm(pl[0:16, :], wlr[:, kc, 16 * z:16 * z + 16], hT[:, kc, n0:n0 + 512], r=['wlr', ('hT', kc)], w=[plk],
                                   start=kc == 0, stop=kc == 7)
                            act(lrt, pl[0:16, :], AF.Copy, r=[plk], w=['lrt'])
                            ps, pk = psr.next()
                            mm(ps, glw[:, z, p2i * 128:(p2i + 1) * 128], lrt, r=['glw', 'lrt'], w=[pk])
                            act(tmpN[:, n0:n0 + 512], ps, AF.Exp, r=[pk, 'ngb'], w=['tmpN'], scale=-1.0,
                                bias=ngb[:, 2 * z + p2i:2 * z + p2i + 1])
                        act(tmpN, tmpN, AF.Ln, r=['tmpN'], w=['tmpN'], bias=1.0)
                        for n0 in range(0, N, 512):
                            k.op('dve', lambda en: en.tensor_tensor_scan(out=cs[:, n0:n0 + 512], data0=M0, data1=tmpN[:, n0:n0 + 512],
                                                                         initial=0.0, op0=ALU.mult, op1=ALU.add),
                                 r=['M0', 'tmpN'], w=['cs'])
                        cs3 = cs.rearrange("p (c t) -> p c t", t=128)
                        tm3 = tmpN.rearrange("p (c t) -> p c t", t=128)
                        if z == 1:
                            cp(tot, cs3[:, :, 127], r=['cs'], w=['tot'])
                            tt(tmpN, tmpN, cs, ALU.subtract, r=['tmpN', 'cs'], w=['tmpN'])
                            tt(cs3, tm3, tot.to_broadcast([128, nch, 128]) if False else tot[:, :, None].to_broadcast([128, nch, 128]),
                               ALU.add, r=['tmpN', 'tot'], w=['cs'])
                        lastc = 127 if z == 0 else 0
                        cp(tot, cs3[:, :, lastc], r=['cs'], w=['tot'])
                        act(qtl, cs, AF.Exp, r=['cs'], w=['qtl'], scale=-1.0 / 16)
                        stt(qtl, qtl, 0.125, qTg, ALU.mult, ALU.mult, r=['qtl', 'qTg'], w=['qtl'])
                        act(ktl, cs, AF.Exp, r=['cs'], w=['ktl'], scale=1.0 / 16)
                        tt(ktl, ktl, kTg, ALU.mult, r=['ktl', 'kTg'], w=['ktl'])
                        kd3 = kdT.rearrange("p (c t) -> p c t", t=128)
                        tt(kd3, cs3, tot[:, :, None].to_broadcast([128, nch, 128]), ALU.subtract, r=['cs', 'tot'], w=['kdT'])
                        act(kdT, kdT, AF.Exp, r=['kdT'], w=['kdT'], scale=1.0 / 16)
                        tt(kdT, kdT, kTg, ALU.mult, r=['kdT', 'kTg'], w=['kdT'])
                        act(ebl, tot, AF.Exp, r=['tot'], w=['ebl'], scale=-1.0 / 16)
                        for si, (q0, T) in enumerate(G['seqs']):
                            c0, ncs = q0 // 128, T // 128
                            for hh in range(2):
                                h = 2 * p2i + hh
                                prs = slice(64 * hh, 64 * hh + 64)
                                if G is GP:
                                    mset(Sg[hh][prs, :], 0.0, w=[('Sg', hh)])
                                else:
                                    k.dma(QA, Sg[hh][prs, :], sgla[l, z, h], w=[('Sg', hh)])
                            for i in range(ncs):
                                s = c0 + i if z == 0 else c0 + ncs - 1 - i
                                sl = slice(s * 128, (s + 1) * 128)
                                ps, pk = psr.next()
                                tr(ps[:, 0:128], kdT[:, sl], ident, r=['kdT'], w=[pk])
                                kd, kdk = kdr.next()
                                act(kd, ps[:, 0:128], AF.Copy, r=[pk], w=[kdk])
                                for hh in range(2):
                                    prs = slice(64 * hh, 64 * hh + 64)
                                    pa, pak = psr.next()
                                    mm(pa[:, 0:128], ktl[prs, sl], qtl[prs, sl], r=['ktl', 'qtl'], w=[pak])
                                    at, atk = atr.next()
                                    tt(at, pa[:, 0:128], MINC[z], ALU.mult, r=[pak, 'const'], w=[atk])
                                    O, Ok = PS[4 + hh], PK[4 + hh]
                                    mm(O[:, 0:128], at, vtg[hh][:, s, :], r=[atk, ('vtg', hh, s)], w=[Ok], start=True, stop=False)
                                    mm(O[:, 0:128], qtl[:, sl], Sg[hh], r=['qtl', ('Sg', hh)], w=[Ok], start=False, stop=True)
                                    if z == 0:
                                        act(oag[hh][:, s, :], O[:, 0:128], AF.Copy, r=[Ok], w=[('oag', hh, s)])
                                    else:
                                        tt(oag[hh][:, s, :], O[:, 0:128], oag[hh][:, s, :], ALU.add, r=[Ok, ('oag', hh, s)], w=[('oag', hh, s)])
                                    S2, S2k = PS[6 + hh], PK[6 + hh]
                                    mm(S2[:, 0:128], kd, vtg[hh][:, s, :], r=[kdk, ('vtg', hh, s)], w=[S2k])
                                    stt(Sg[hh][prs, :], Sg[hh][prs, :], ebl[prs, s:s + 1], S2[prs, 0:128], ALU.mult, ALU.add,
                                        r=[('Sg', hh), 'ebl', S2k], w=[('Sg', hh)])
                            if G is GP:
                                for hh in range(2):
                                    h = 2 * p2i + hh
                                    k.dma(QA, ngla[si, l, z, h], Sg[hh][64 * hh:64 * hh + 64, :], r=[('Sg', hh)])
                    for hh in range(2):
                        h = 2 * p2i + hh

                        def evg(ps, pk, n0):
                            act(sgg[:, n0:n0 + 512], ps, AF.Silu, r=[pk], w=['sgg'])
                        proj_fm(wring, C_GG + h * 128, 128, evg)
                        for s in range(nch):
                            sq = ssg[:, (s % 4):(s % 4) + 1]; sqk = ('ssg', s % 4)
                            act(junkg, oag[hh][:, s, :], AF.Square, r=[('oag', hh, s)], w=['junkg', sqk], accum=sq)
                            act(sq, sq, AF.Sqrt, r=[sqk], w=[sqk], scale=1.0 / 128, bias=EPS)
                            recip(sq, sq, r=[sqk], w=[sqk])
                            on, onk = onr.next()
                            ts(on, oag[hh][:, s, :], sq, ALU.mult, r=[('oag', hh, s), sqk], w=[onk])
                            ps, pk = psr.next()
                            tr(ps[:, 0:128], on, ident, r=[onk], w=[pk])
                            stt(oTg[:, s * 128:(s + 1) * 128], ps[:, 0:128], P[:, R_GLAG:R_GLAG + 1], sgg[:, s * 128:(s + 1) * 128],
                                ALU.mult, ALU.mult, r=[pk, 'pT', 'sgg'], w=['cs', 'oTgb'])
                        k.dma(QA, OSC[2][h * 128:(h + 1) * 128, t0:t0 + N], oTg, r=['cs', 'oTgb'], w=[('osc', 2, G['name'], h)])
                gl.close()

            if STAGE >= 4:
                mg = Scope(k)
                wbr_r = Ring([mg.sb("wbr_%d" % i, [128, 4, 128], MMDT) for i in range(3)], 'wbr')
                wm_r = Ring([mg.sb("wm_%d" % i, [128, 8, 128], MMDT) for i in range(3)], 'wm')
                wo = [mg.sb("wo_%d" % i, [128, 8, 512], MMDT) for i in range(2)]
                ob = [mg.sb("ob_%d" % i, [128, 4, 512], MMDT) for i in range(3)]
                mT = mg.sb("mT", [128, 8, 512], MMDT)
                mTf = mg.sb("mTf", [128, 8, 512]) if BF else mT
                sgr = Ring([mg.sb("sg_%d" % i, [128, 512]) for i in range(2)], 'sg')
                tmr = Ring([mg.sb("tm_%d" % i, [128, 512]) for i in range(2)], 'tm')
                xr = Ring([mg.sb("xm_%d" % i, [128, D]) for i in range(2)], 'xm')
                xo_r = Ring([mg.sb("xo_%d" % i, [128, D]) for i in range(2)], 'xo')
                for n0 in range(0, N, 512):
                    for br in range(3):
                        k.dma(QA, ob[br], OSC[br][:, t0 + n0:t0 + n0 + 512].rearrange("(kc p) t -> p kc t", p=128),
                              r=[('osc', br, G['name'])], w=[('ob', br)])
                    for half in range(2):
                        if not BF or SWCAST:
                            k.dma(QW, wo[half], w_out[l, :, half * 512:(half + 1) * 512].rearrange("(kc p) m -> p kc m", p=128),
                                  w=[('wo', half)])
                            continue
                        for q4 in range(4):
                            c0_ = half * 512 + q4 * 128
                            load_w(wo[half][:, :, q4 * 128:(q4 + 1) * 128], ('wo', half, q4),
                                   w_out[l, :, c0_:c0_ + 128].rearrange("(kc p) m -> p kc m", p=128), 8, 128)
                    for oc in range(8):
                        for br in range(3):
                            wb, wbk = wbr_r.next()
                            load_w(wb, wbk, w_br[l, br, :, oc * 128:(oc + 1) * 128].rearrange("(kc p) m -> p kc m", p=128), 4, 128)
                            pp, ppk = psr.next()
                            for kc in range(4):
                                mm(pp, wb[:, kc, :], ob[br][:, kc, :], r=[wbk, ('ob', br)], w=[ppk], start=kc == 0, stop=kc == 3)
                            wt, wk = wtile_load(wm_r, w_in[l, :, C_M[br] + oc * 128:C_M[br] + (oc + 1) * 128], 128)
                            pm, pmk = psr.next()
                            for kc in range(8):
                                mm(pm, wt[:, kc, :], hT[:, kc, n0:n0 + 512], r=[wk, ('hT', kc)], w=[pmk], start=kc == 0, stop=kc == 7)
                            sg, sgk = sgr.next()
                            act(sg, pm, AF.Sigmoid, r=[pmk], w=[sgk])
                            if br == 0:
                                tt(mTf[:, oc, :], pp, sg, ALU.mult, r=[ppk, sgk], w=[('mTf', oc)])
                            else:
                                tm, tmk = tmr.next()
                                tt(tm, pp, sg, ALU.mult, r=[ppk, sgk], w=[tmk])
                                dst = mT if br == 2 else mTf
                                tt(dst[:, oc, :], mTf[:, oc, :], tm, ALU.add, r=[('mTf', oc), tmk], w=[('mTf', oc), ('mT', oc)], e=PE2)
                    for sub in range(4):
                        s = n0 // 128 + sub
                        xt, xk = xr.next()
                        k.dma(QA, xt, xsrc[s * 128:(s + 1) * 128, :], r=[('y', G['name'], s)], w=[xk])
                        xo, xok = xo_r.next()
                        for half in range(2):
                            pp, ppk = psr.next()
                            for kc in range(8):
                                mm(pp, mT[:, kc, sub * 128:(sub + 1) * 128], wo[half][:, kc, :], r=[('mT', kc), ('mTf', kc), ('wo', half)], w=[ppk],
                                   start=kc == 0, stop=kc == 7)
                            tt(xo[:, half * 512:(half + 1) * 512], pp, gtb[:, 0, cond, half * 512:(half + 1) * 512], ALU.mult,
                               r=[ppk, 'gtb'], w=[(xok[0], xok[1], half)])
                        tt(xo, xo, xt, ALU.add, r=[xok, xk], w=[xok], e=PE2)
                        k.dma(QA, G['y'][s * 128:(s + 1) * 128, :], xo, r=[xok], w=[('y', G['name'], s)])
                mg.close()
            gs.close()

        if STAGE >= 5:
            ff = Scope(k)
            wu_r = Ring([ff.sb("wu_%d" % i, [128, 8, 128], MMDT) for i in range(4)], 'wu')
            wd_r = Ring([ff.sb("wd_%d" % i, [128, D], MMDT) for i in range(4)], 'wd')
            h2 = ff.sb("h2", [128, 8, 514], MMDT)
            actT = ff.sb("actT", [128, 22, 512], MMDT)
            xm = ff.sb("xmf", [128, 4, D]); xh_all = ff.sb("xh", [2, 4, D]); xhn = ff.sb("xhn", [2, D])
            xn_r = Ring([ff.sb("xnf_%d" % i, [128, D]) for i in range(2)], 'xnf')
            junkf = ff.sb("junkf", [128, D]); ssf = ff.sb("ssf", [128, 4]); ssh = ff.sb("ssh", [2, 1])
            cu_r = Ring([ff.sb("cu_%d" % i, [128, 512]) for i in range(2)], 'cu')
            sgf = ff.sb("sgf", [128, 512])
            xo_r = Ring([ff.sb("xof_%d" % i, [128, D]) for i in range(2)], 'xof')
            tiles = []
            last = (l == 1)
            for G in (GP, GS):
                for n0 in range(0, G['N'], 512):
                    if G is GP:
                        tiles.append((G, n0, [(0, 256), (256, 256)], False, False, False))
                    elif not last:
                        tiles.append((G, n0, [(0, 512)], n0 > 0, n0 + 512 < G['N'], False))
                    elif n0 == 0:
                        tiles.append((G, n0, [(0, 512)], True, True, True))
            if not last:
                for ti in range(4):
                    n0 = 512 * ti
                    lrow = n0 - 1 if ti > 0 else n0
                    rrow = n0 + 512 if ti < 3 else n0
                    k.dma(QA, xh_all[0:1, ti, :], ys[lrow:lrow + 1, :], r=[('y', 'S', lrow // 128)], w=[('xh', ti)])
                    k.dma(QA, xh_all[1:2, ti, :], ys[rrow:rrow + 1, :], r=[('y', 'S', rrow // 128)], w=[('xh', ti)])
            else:
                w0h = ff.sb("w0h", [128, 44]); w2h = ff.sb("w2h", [128, 44])
                ts(w0h, P[:, R_FCONV:R_FCONV + 44], selc[:, 4:5], ALU.mult, r=['pT', 'selc'], w=['w0h'])
                ts(w2h, P[:, R_FCONV + 88:R_FCONV + 132], selc[:, 5:6], ALU.mult, r=['pT', 'selc'], w=['w2h'])
                for i in range(3):
                    k.dma(QA, xh_all[0:1, i, :], ys[(i + 1) * 512 - 1:(i + 1) * 512, :], r=[('y', 'S', (i + 1) * 4 - 1)], w=[('xh', i)])
                    k.dma(QA, xh_all[1:2, i, :], ys[(i + 1) * 512:(i + 1) * 512 + 1, :], r=[('y', 'S', (i + 1) * 4)], w=[('xh', i)])
                ts(xh_all[:, 3, :], xh_all[:, 0, :], selc[0:2, 8:9], ALU.mult, r=[('xh', 0), 'selc'], w=[('xh', 3)])
                for i in (1, 2):
                    stt(xh_all[:, 3, :], xh_all[:, i, :], selc[0:2, 8 + i:9 + i], xh_all[:, 3, :], ALU.mult, ALU.add,
                        r=[('xh', i), ('xh', 3), 'selc'], w=[('xh', 3)])
            for (G, n0, segs, hl, hr, own) in tiles:
                cond = G['cond']
                yv = G['y']
                xh = xh_all[:, 3 if own else n0 // 512, :]
                xhk = ('xh', 3 if own else n0 // 512)
                if not own:
                    k.dma(QA, xm, yv[n0:n0 + 512, :].rearrange("(s p) d -> p s d", p=128),
                          r=[('y', G['name'], n0 // 128 + i) for i in range(4)], w=['xmf'])
                else:
                    for sub in range(4):
                        for r_ in range(4):
                            xb_, xbk = xo_r.next()
                            row0 = r_ * 512 + sub * 128
                            k.dma(QA, xb_, ys[row0:row0 + 128, :], r=[('y', 'S', row0 // 128)], w=[xbk])
                            if r_ == 0:
                                ts(xm[:, sub, :], xb_, selc[:, 0:1], ALU.mult, r=[xbk, 'selc'], w=[('xmf', sub)])
                            else:
                                stt(xm[:, sub, :], xb_, selc[:, r_:r_ + 1], xm[:, sub, :], ALU.mult, ALU.add,
                                    r=[xbk, 'selc', ('xmf', sub)], w=[('xmf', sub)])
                halo = hl or hr
                if halo:
                    act(junkf[0:2, :], xh, AF.Square, r=[xhk], w=['junkf', 'ssh'], accum=ssh)
                    act(ssh, ssh, AF.Sqrt, r=['ssh'], w=['ssh'], scale=1.0 / D, bias=EPS)
                    recip(ssh, ssh, r=['ssh'], w=['ssh'])
                    ts(xhn, xh, ssh, ALU.mult, r=[xhk, 'ssh'], w=['xhn'])
                    for half in range(2):
                        ps, pk = psr.next()
                        for j in range(4):
                            kc = half * 4 + j
                            tr(ps[:, j * 2:j * 2 + 2], xhn[0:2, kc * 128:(kc + 1) * 128], ident[0:2, 0:2], r=['xhn'], w=[pk])
                        for j in range(4):
                            kc = half * 4 + j
                            act(h2[:, kc, 512:514], ps[:, j * 2:j * 2 + 2], AF.Identity, r=[pk, 'a_sc', 'modF'], w=[('h2', kc, 'h')],
                                scale=a_sc[:, 1, kc, cond:cond + 1], bias=modF[:, 2, kc, cond:cond + 1])
                for sub in range(4):
                    sq = ssf[:, sub:sub + 1]; sqk = ('ssf', sub)
                    act(junkf, xm[:, sub, :], AF.Square, r=['xmf'], w=['junkf', sqk], accum=sq)
                    act(sq, sq, AF.Sqrt, r=[sqk], w=[sqk], scale=1.0 / D, bias=EPS)
                    recip(sq, sq, r=[sqk], w=[sqk])
                    xn, xnk = xn_r.next()
                    ts(xn, xm[:, sub, :], sq, ALU.mult, r=['xmf', sqk], w=[xnk])
                    for half in range(2):
                        ps, pk = psr.next()
                        for j in range(4):
                            kc = half * 4 + j
                            tr(ps[:, j * 128:(j + 1) * 128], xn[:, kc * 128:(kc + 1) * 128], ident, r=[xnk], w=[pk])
                        for j in range(4):
                            kc = half * 4 + j
                            act(h2[:, kc, sub * 128:(sub + 1) * 128], ps[:, j * 128:(j + 1) * 128], AF.Identity,
                                r=[pk, 'a_sc', 'modF'], w=[('h2', kc, sub)],
                                scale=a_sc[:, 1, kc, cond:cond + 1], bias=modF[:, 2, kc, cond:cond + 1])

                def up_chunk(j):
                    wt, wk = wtile_load(wu_r, w_up[l, :, j * 128:(j + 1) * 128], 128)
                    pu, puk = psr.next()
                    for kc in range(8):
                        mm(pu, wt[:, kc, :], h2[:, kc, 0:512], r=[wk, ('h2', kc)], w=[puk], start=kc == 0, stop=kc == 7)
                    if halo:
                        phh, phk = psr.next()
                        for kc in range(8):
                            mm(phh[:, 0:2], wt[:, kc, :], h2[:, kc, 512:514], r=[wk, ('h2', kc)], w=[phk], start=kc == 0, stop=kc == 7)
                    w0 = P[:, R_FCONV + j:R_FCONV + j + 1]
                    w1 = P[:, R_FCONV + 44 + j:R_FCONV + 44 + j + 1]
                    w2 = P[:, R_FCONV + 88 + j:R_FCONV + 88 + j + 1]
                    cu, cuk = cu_r.next()
                    act(cu, pu, AF.Identity, r=[puk, 'pT'], w=[cuk], scale=w1)
                    for (q0, T) in segs:
                        stt(cu[:, q0 + 1:q0 + T], pu[:, q0:q0 + T - 1], w0, cu[:, q0 + 1:q0 + T], ALU.mult, ALU.add, r=[puk, 'pT', cuk], w=[cuk])
                        stt(cu[:, q0:q0 + T - 1], pu[:, q0 + 1:q0 + T], w2, cu[:, q0:q0 + T - 1], ALU.mult, ALU.add, r=[puk, 'pT', cuk], w=[cuk])
                    w0e = w0h[:, j:j + 1] if own else w0
                    w2e = w2h[:, j:j + 1] if own else w2
                    if hl:
                        stt(cu[:, 0:1], phh[:, 0:1], w0e, cu[:, 0:1], ALU.mult, ALU.add, r=[phk, 'pT', 'w0h', cuk], w=[cuk])
                    if hr:
                        stt(cu[:, 511:512], phh[:, 1:2], w2e, cu[:, 511:512], ALU.mult, ALU.add, r=[phk, 'pT', 'w2h', cuk], w=[cuk])
                    return cu, cuk
                for jj in range(22):
                    cu, cuk = up_chunk(22 + jj)
                    act(sgf, cu, AF.Silu, r=[cuk], w=['sgf'])
                    cu, cuk = up_chunk(jj)
                    tt(actT[:, jj, :], cu, sgf, ALU.mult, r=[cuk, 'sgf'], w=[('actT', jj)], e=PE2)
                for jj in range(22):
                    wd, wdk = wd_r.next()
                    load_w(wd.rearrange("p (a b) -> p a b", b=128), wdk, w_dn[l, jj * 128:(jj + 1) * 128, :].rearrange("p (a b) -> p a b", b=128), 8, 128)
                    for sub in range(4):
                        for half in range(2):
                            b = sub * 2 + half
                            mm(PS[b], actT[:, jj, sub * 128:(sub + 1) * 128], wd[:, half * 512:(half + 1) * 512],
                               r=[('actT', jj), wdk], w=[PK[b]], start=jj == 0, stop=jj == 21)
                for sub in range(4):
                    s = n0 // 128 + sub
                    xo, xok = xo_r.next()
                    for half in range(2):
                        b = sub * 2 + half
                        tt(xo[:, half * 512:(half + 1) * 512], PS[b], gtb[:, 1, cond, half * 512:(half + 1) * 512], ALU.mult,
                           r=[PK[b], 'gtb'], w=[(xok[0], xok[1], half)])
                    tt(xo, xo, xm[:, sub, :], ALU.add, r=[xok, 'xmf'], w=[xok], e=PE2)
                    if own:
                        k.dma(QA, ys_own[sub * 128:(sub + 1) * 128, :], xo, r=[xok], w=[('yown', sub)])
                    else:
                        k.dma(QA, yv[s * 128:(s + 1) * 128, :], xo, r=[xok], w=[('y', G['name'], s)])
            ff.close()
        lay.close()
    top.close()
    return nc


def _consts():
    i = np.arange(128)
    ident = np.eye(128, dtype=np.float32)
    ones = np.ones((128, 128), np.float32)
    blk = (i[:, None] // 64 == i[None, :] // 64).astype(np.float32)
    tu = (i[:, None] <= i[None, :]).astype(np.float32)
    tl = (i[:, None] >= i[None, :]).astype(np.float32)
    su = (i[:, None] < i[None, :]).astype(np.float32)
    sl = (i[:, None] > i[None, :]).astype(np.float32)
    return np.ascontiguousarray(np.concatenate([ident, ones, blk, tu, tl, su, sl], axis=1))


def _rtab(na_rpb):
    p = np.arange(128)
    a, kc = p // 64, p % 64
    jp = np.arange(22)
    qc = np.arange(64)
    delta = a[:, None] + 10 - jp[None, :]
    dvalid = np.abs(delta) <= 7
    dr = np.clip(delta + 7, 0, 14)
    dcol = kc[:, None] - qc[None, :]
    ws = np.clip(qc - 8, 0, 48)
    cvalid = (kc[:, None] >= ws[None, :]) & (kc[:, None] < ws[None, :] + 16)
    dc = np.clip(dcol, -15, 15) + 15
    g = na_rpb[:, :, dr[:, :, None], dc[:, None, :]]
    out = np.where(cvalid[None, None, :, None, :], g, np.float32(NEG))
    out = np.where(dvalid[None, None, :, :, None], out, np.float32(0.0))
    return np.ascontiguousarray(out.reshape(2, 8, 128, 22 * 64).astype(np.float32))


def _pack_params(norm1, norm2, b_ada, dn_conv, ffn_conv, qg, kg, dng, glag, glab):
    pf = np.zeros((2, 256, 128), np.float32)
    for l in range(2):
        pf[l, R_N1:R_N1 + 8] = norm1[l].reshape(8, 128)
        pf[l, R_N2:R_N2 + 8] = norm2[l].reshape(8, 128)
        pf[l, R_BADA:R_BADA + 48] = b_ada[l].reshape(48, 128)
        pf[l, R_DNCONV:R_DNCONV + 36] = dn_conv[l].reshape(36, 128)
        pf[l, R_FCONV:R_FCONV + 132] = ffn_conv[l].reshape(132, 128)
        pf[l, R_QG] = np.tile(qg[l], 2)
        pf[l, R_KG] = np.tile(kg[l], 2)
        pf[l, R_DNG] = dng[l]
        pf[l, R_GLAG] = glag[l]
        pf[l, R_GLAB:R_GLAB + 4] = glab[l].reshape(4, 128)
    return pf


_PROG = None


def kernel(x_prompt, x_sample, cache_k, cache_v, state_dn, state_gla, c, c_ctx,
           w_ada, b_ada, norm1, w_in, na_q_norm, na_k_norm, na_rpb,
           dn_conv, dn_a_log, dn_dt_bias, dn_out_norm,
           gla_w_gate, gla_b_gate, gla_out_norm,
           w_branch, w_out, norm2, w_up, ffn_conv, w_down):
    global _PROG
    f = lambda a: np.ascontiguousarray(np.asarray(a, dtype=np.float32))
    x_prompt, x_sample = f(x_prompt), f(x_sample)
    if _PROG is None:
        _PROG = build_program()
    nc = _PROG
    pf = _pack_params(f(norm1), f(norm2), f(b_ada), f(dn_conv), f(ffn_conv), f(na_q_norm), f(na_k_norm),
                      f(dn_out_norm), f(gla_out_norm), f(gla_b_gate))
    rt = _rtab(f(na_rpb))
    dnab = np.ascontiguousarray(np.stack([f(dn_a_log).reshape(2, 8), f(dn_dt_bias).reshape(2, 8)], axis=1))
    consts = _consts()
    shared = dict(w_ada=f(w_ada), b_ada=f(b_ada), w_in=f(w_in), w_br=f(w_branch), w_out=f(w_out), w_up=f(w_up),
                  w_dn=f(w_down), pf=pf, rtab=rt, dnab=dnab, glaw=f(gla_w_gate), consts=consts)
    in_maps = []
    for core in range(8):
        sb = core // 4
        m = dict(shared)
        m['xp'] = np.ascontiguousarray(x_prompt[4 * core:4 * core + 4].reshape(NP_TOK, D))
        m['xs'] = np.ascontiguousarray(x_sample[sb])
        m['ck'] = np.ascontiguousarray(f(cache_k)[sb].reshape(2, 512, 512))
        m['cv'] = np.ascontiguousarray(f(cache_v)[sb].reshape(2, 512, 512))
        m['sdn'] = np.ascontiguousarray(f(state_dn)[sb])
        m['sgla'] = np.ascontiguousarray(f(state_gla)[sb])
        m['cvec'] = np.ascontiguousarray(np.concatenate([f(c_ctx).reshape(8, 128), f(c)[sb].reshape(8, 128)], axis=0))
        rk = core % 4
        selc = np.zeros((128, 16), np.float32)
        selc[:, rk] = 1.0
        selc[:, 4] = 0.0 if rk == 0 else 1.0
        selc[:, 5] = 0.0 if rk == 3 else 1.0
        for i in range(3):
            selc[0, 8 + i] = 1.0 if rk == i + 1 else 0.0
            selc[1, 8 + i] = 1.0 if rk == i else 0.0
        m['selc'] = selc
        in_maps.append(m)
    res = run_bass_kernel_spmd(nc, in_maps, core_ids=list(range(8)))
    R = res.results
    y_p = np.concatenate([R[i]['yp'].reshape(4, 256, D) for i in range(8)], axis=0)
    y_s = np.stack([np.concatenate([R[4 * b + j]['ys_own'] for j in range(4)], axis=0) for b in range(2)], axis=0)
    nk_ = np.concatenate([R[i]['nk'].reshape(4, 2, 256, 8, 64) for i in range(8)], axis=0)
    nv_ = np.concatenate([R[i]['nv'].reshape(4, 2, 256, 8, 64) for i in range(8)], axis=0)
    ndn_ = np.concatenate([R[i]['ndn'] for i in range(8)], axis=0)
    ngla_ = np.concatenate([R[i]['ngla'] for i in range(8)], axis=0)
    return (y_p.astype(np.float32), y_s.astype(np.float32), nk_.astype(np.float32), nv_.astype(np.float32),
            ndn_.astype(np.float32), ngla_.astype(np.float32))
```

```python
import os
from contextlib import ExitStack
import numpy as np
import concourse.bass as bass
import concourse.mybir as mybir
from concourse.bass_utils import run_bass_kernel_spmd

F32 = mybir.dt.float32
AF = mybir.ActivationFunctionType
ALU = mybir.AluOpType

EPS = 1e-6
D = 1024
NP_TOK = 1024
NS_TOK = 2048
NTOK = NP_TOK + NS_TOK
PROJ = 8240
C_NAQ, C_NAK, C_NAV = 0, 512, 1024
C_DNQ, C_DNK, C_DNV, C_DNG = 1536, 2048, 2560, 3072
C_DNB = 3584
C_GQ, C_GK, C_GV, C_GG, C_GLR = 3600, 3856, 4112, 4624, 5136
C_M = (5168, 6192, 7216)
NEG = -30000.0
R_N1, R_N2, R_BADA, R_DNCONV, R_FCONV, R_QG, R_KG, R_DNG, R_GLAG, R_GLAB = 0, 8, 16, 64, 100, 232, 233, 234, 235, 236
K_ID, K_ONES, K_BLK, K_TU, K_TL, K_SU, K_SL = [i * 128 for i in range(7)]
NDMA_SEMS = 20
STAGE = int(os.environ.get("KSTAGE", "99"))
BF = bool(int(os.environ.get("KBF", "1")))
BF16 = mybir.dt.bfloat16
KDBG = int(os.environ.get("KDBG", "0"))
MMDT = BF16 if BF else F32
SWCAST = BF and bool(int(os.environ.get("KSWCAST", "1")))
QW = 'pool' if SWCAST else 'sp'
QA = 'sp' if SWCAST else 'pool'
PE2 = 'dve' if SWCAST else 'pool'


class K:
    def __init__(self, nc):
        self.nc = nc
        self.eng = {'pe': nc.tensor, 'act': nc.scalar, 'dve': nc.vector, 'pool': nc.gpsimd, 'sp': nc.sync}
        self.sem = {e: nc.alloc_semaphore('s_' + e) for e in ('pe', 'act', 'dve', 'pool')}
        self.cnt = {e: 0 for e in self.sem}
        self.waited = {e: {} for e in self.eng}
        self.res = {}
        self.children = {}
        self.dq = {}
        for q in ('sp', 'pool'):
            self.dq[q] = dict(sems=[nc.alloc_semaphore('d_%s%d' % (q, i)) for i in range(NDMA_SEMS)], n=0)
        self.n_inst = 0
        self.uid = 0

    def _wait(self, e, tok):
        if tok is None:
            return
        if tok[0] == 'e':
            src, val, sem = ('e', tok[1]), tok[2], self.sem[tok[1]]
        else:
            src, val, sem = ('d', tok[1], tok[2]), tok[3], self.dq[tok[1]]['sems'][tok[2]]
        if self.waited[e].get(src, 0) >= val:
            return
        self.eng[e].wait_ge(sem, val)
        self.waited[e][src] = val

    def _related(self, key):
        out = []
        for i in range(1, len(key) + 1):
            p = key[:i]
            if p in self.res:
                out.append(p)
        for c in self.children.get(key, ()):
            if c != key:
                out.append(c)
        return out

    def _get(self, key):
        if key not in self.res:
            self.res[key] = dict(w=None, r=[])
            for i in range(1, len(key) + 1):
                self.children.setdefault(key[:i], set()).add(key)
        return self.res[key]

    def _deps(self, e, r, w, acc=False):
        for key in r:
            for kk in self._related(key):
                self._wait(e, self.res[kk]['w'])
        for key in w:
            for kk in self._related(key):
                ent = self.res[kk]
                if ent['r']:
                    for t in ent['r']:
                        self._wait(e, t)
                elif not (acc and e == 'pe' and ent['w'] is not None and ent['w'][:2] == ('e', 'pe')):
                    self._wait(e, ent['w'])

    def _record(self, tok, r, w):
        src = tok[:2] if tok[0] == 'e' else tok[:3]
        for key in r:
            ent = self._get(key)
            ent['r'] = [t for t in ent['r'] if (t[:2] if t[0] == 'e' else t[:3]) != src]
            ent['r'].append(tok)
        for key in w:
            self._get(key)
            for kk in self._related(key) + [key]:
                self.res[kk]['w'] = tok
                self.res[kk]['r'] = []

    @staticmethod
    def _keys(ks):
        return [k if isinstance(k, tuple) else (k,) for k in ks]

    def op(self, e, fn, r=(), w=(), acc=False):
        r, w = self._keys(r), self._keys(w)
        self._deps(e, r, w, acc=acc)
        inst = fn(self.eng[e])
        self.cnt[e] += 1
        inst.then_inc(self.sem[e], 1)
        self._record(('e', e, self.cnt[e]), r, w)
        self.n_inst += 1
        return inst

    def dma(self, q, out, in_, r=(), w=(), **kw):
        r, w = self._keys(r), self._keys(w)
        d = self.dq[q]
        n = d['n']
        slot = n % NDMA_SEMS
        val = 16 * (n // NDMA_SEMS + 1)
        if n >= NDMA_SEMS:
            self._wait(q, ('d', q, slot, val - 16))
        self._deps(q, r, w)
        inst = self.eng[q].dma_start(out=out, in_=in_, **kw)
        inst.then_inc(d['sems'][slot], 16)
        d['n'] = n + 1
        self._record(('d', q, slot, val), r, w)
        self.n_inst += 1
        return inst

    def barrier(self, keep=()):
        toks = [('e', e, self.cnt[e]) for e in self.cnt if self.cnt[e] > 0]
        for q, d in self.dq.items():
            n = d['n']
            for slot in range(min(n, NDMA_SEMS)):
                last = ((n - 1 - slot) // NDMA_SEMS) * NDMA_SEMS + slot
                toks.append(('d', q, slot, 16 * (last // NDMA_SEMS + 1)))
        for e in ('pe', 'act', 'dve', 'pool', 'sp'):
            for t in toks:
                self._wait(e, t)
        self.res = {}
        self.children = {}


class Scope:
    def __init__(self, k):
        self.k = k
        self.es = ExitStack()

    def sb(self, name, shape, dt=F32):
        self.k.uid += 1
        return self.es.enter_context(self.k.nc.sbuf_tensor("%s_%d" % (name, self.k.uid), list(shape), dt)).ap()

    def close(self):
        self.k.barrier()
        self.es.close()


class Ring:
    def __init__(self, tiles, name, keys=None):
        self.t, self.name, self.i, self.keys = tiles, name, 0, keys

    def next(self):
        i = self.i % len(self.t)
        self.i += 1
        return self.t[i], (self.keys[i] if self.keys else (self.name, i))


def run_rr(gens):
    gens = list(gens)
    while gens:
        nxt = []
        for g in gens:
            try:
                next(g)
                nxt.append(g)
            except StopIteration:
                pass
        gens = nxt


def build_program():
    nc = bass.Bass("TRN2", target_bir_lowering=False)
    k = K(nc)

    def din(name, shape):
        return nc.dram_tensor(name, list(shape), F32, kind="ExternalInput").ap()

    def dout(name, shape):
        return nc.dram_tensor(name, list(shape), F32, kind="ExternalOutput").ap()

    xp = din("xp", [NP_TOK, D]); xs = din("xs", [NS_TOK, D])
    ck = din("ck", [2, 512, 512]); cv = din("cv", [2, 512, 512])
    sdn = din("sdn", [2, 2, 4, 128, 128]); sgla = din("sgla", [2, 2, 4, 64, 128])
    cvec = din("cvec", [16, 128])
    w_ada = din("w_ada", [2, D, 6144]); b_ada = din("b_ada", [2, 6144])
    w_in = din("w_in", [2, D, PROJ]); w_br = din("w_br", [2, 3, 512, D]); w_out = din("w_out", [2, D, D])
    w_up = din("w_up", [2, D, 5632]); w_dn = din("w_dn", [2, 2816, D])
    pf = din("pf", [2, 256, 128]); rtab = din("rtab", [2, 8, 128, 1408])
    dnab = din("dnab", [2, 2, 8]); glaw = din("glaw", [2, 2, 16, 256]); consts_d = din("consts", [128, 896])
    selc_d = din("selc", [128, 16])
    yp = dout("yp", [NP_TOK, D]); ys = dout("ys", [NS_TOK, D]); ys_own = dout("ys_own", [512, D])
    nk = dout("nk", [4, 2, 256, 512]); nv = dout("nv", [4, 2, 256, 512])
    ndn = dout("ndn", [4, 2, 2, 4, 128, 128]); ngla = dout("ngla", [4, 2, 2, 4, 64, 128])
    OSC = [nc.dram_tensor("osc%d" % i, [512, NTOK], MMDT, kind="Internal").ap() for i in range(3)]

    PS = [nc.alloc_psum_tensor("psb%d" % i, [128, 512], F32).ap() for i in range(8)]
    psr = Ring(PS[0:4], 'ps')
    PK = [('ps', i) for i in range(4)] + [('psx', i) for i in range(4, 8)]
    psr8 = Ring(PS, 'ps8', keys=PK)

    top = Scope(k)
    CONST = top.sb("const", [128, 896])
    pT = [top.sb("pT%d" % l, [128, 256]) for l in range(2)]
    scT = top.sb("scT", [128, 16])
    k.dma('sp', CONST, consts_d, w=['const'])
    selc = top.sb("selc", [128, 16])
    k.dma('sp', selc, selc_d, w=['selc'])
    ident = CONST[:, K_ID:K_ID + 128]; ones = CONST[:, K_ONES:K_ONES + 128]; blk = CONST[:, K_BLK:K_BLK + 128]
    UCUM = (CONST[:, K_TU:K_TU + 128], CONST[:, K_TL:K_TL + 128])
    MINC = UCUM
    MSTR = (CONST[:, K_SL:K_SL + 128], CONST[:, K_SU:K_SU + 128])

    def act(out, in_, func, r, w, bias=None, scale=None, accum=None):
        kw = {}
        if bias is not None: kw['bias'] = bias
        if scale is not None: kw['scale'] = scale
        if accum is not None: kw['accum_out'] = accum
        return k.op('act', lambda e: e.activation(out=out, in_=in_, func=func, **kw), r=r, w=w)

    def mm(out, lhsT, rhs, r, w, start=True, stop=True):
        return k.op('pe', lambda e: e.matmul(out, lhsT=lhsT, rhs=rhs, start=start, stop=stop), r=r, w=w, acc=not start)

    def tr(out, in_, idn, r, w):
        return k.op('pe', lambda e: e.transpose(out, in_, idn), r=r + ['const'], w=w)

    def tt(out, in0, in1, op, r, w, e='dve'):
        return k.op(e, lambda en: en.tensor_tensor(out=out, in0=in0, in1=in1, op=op), r=r, w=w)

    def ts(out, in0, s1, op0, r, w, s2=None, op1=None, e='dve'):
        if op1 is None:
            return k.op(e, lambda en: en.tensor_scalar(out=out, in0=in0, scalar1=s1, scalar2=None, op0=op0), r=r, w=w)
        return k.op(e, lambda en: en.tensor_scalar(out=out, in0=in0, scalar1=s1, scalar2=s2, op0=op0, op1=op1), r=r, w=w)

    def stt(out, in0, sc, in1, op0, op1, r, w):
        return k.op('dve', lambda en: en.scalar_tensor_tensor(out=out, in0=in0, scalar=sc, in1=in1, op0=op0, op1=op1), r=r, w=w)

    def recip(out, in_, r, w):
        return k.op('dve', lambda en: en.reciprocal(out=out, in_=in_), r=r, w=w)

    def cp(out, in_, r, w, e='dve'):
        return k.op(e, lambda en: en.tensor_copy(out=out, in_=in_), r=r, w=w)

    def mset(ap, val, w, e=PE2):
        return k.op(e, lambda en: en.memset(ap, val), w=w)

    s0 = Scope(k)
    pfs = s0.sb("pfs", [128, 2, 2, 128])
    cvs = s0.sb("cvs", [16, 128])
    k.dma('sp', pfs, pf.rearrange("l (a p) c -> p l a c", p=128), w=['pfs'])
    k.dma('sp', cvs, cvec, w=['cvs'])
    for l in range(2):
        for a in range(2):
            ps, pk = psr.next()
            tr(ps[:, 0:128], pfs[:, l, a, :], ident, r=['pfs'], w=[pk])
            cp(pT[l][:, a * 128:(a + 1) * 128], ps[:, 0:128], r=[pk], w=['pT'])
    ps, pk = psr.next()
    tr(ps[:, 0:16], cvs[0:16, :], ident[0:16, 0:16], r=['cvs'], w=[pk])
    act(scT, ps[:, 0:16], AF.Silu, r=[pk], w=['scT'])
    s0.close()

    GP = dict(name='P', t0=0, N=NP_TOK, seqs=[(i * 256, 256) for i in range(4)], cond=0, xin=xp, y=yp)
    GS = dict(name='S', t0=NP_TOK, N=NS_TOK, seqs=[(0, 2048)], cond=1, xin=xs, y=ys)

    wstage = Ring([top.sb("wst_%d" % i, [128, 8, 128]) for i in range(3)], 'wst') if (BF and not SWCAST) else None
    cast_i = [0]

    def load_w(dst, dkey, src3, a, b):
        if not BF or SWCAST:
            k.dma(QW, dst, src3, w=[dkey])
            return
        st, stk = wstage.next()
        k.dma(QW, st[:, 0:a, 0:b], src3, w=[stk])
        cast_i[0] += 1
        if cast_i[0] % 2:
            cp(dst, st[:, 0:a, 0:b], r=[stk], w=[dkey])
        else:
            act(dst, st[:, 0:a, 0:b], AF.Copy, r=[stk], w=[dkey])

    def wtile_load(ring, src, M, nk=8, q=QW, cast=True):
        wt, wk = ring.next()
        src3 = src.rearrange("(kc p) m -> p kc m", p=128)
        if cast:
            load_w(wt[:, 0:nk, 0:M], wk, src3, nk, M)
        else:
            k.dma(q, wt[:, 0:nk, 0:M], src3, w=[wk])
        return wt, wk

    for l in range(2):
        lay = Scope(k)
        modF = lay.sb("modF", [128, 4, 8, 2])
        a_sc = lay.sb("a_sc", [128, 2, 8, 2])
        gtb = lay.sb("gtb", [128, 2, 2, D])
        qgs = lay.sb("qgs", [128, 1])
        ngb = lay.sb("ngb", [128, 4])
        dtb = lay.sb("dtb", [128, 8]); nexpA = lay.sb("nexpA", [128, 8])
        glw = lay.sb("glw", [16, 2, 256])
        P = pT[l]
        ph = Scope(k)
        wring = Ring([ph.sb("w0_%d" % i, [128, 8, 128]) for i in range(3)], 'w0')
        bb = ph.sb("bb", [128, 2, D])
        scB = ph.sb("scB", [128, 16, 128])
        for i in range(16):
            cp(scB[:, i, :], scT[:, i:i + 1].to_broadcast([128, 128]), r=['scT'], w=['scB'])
        for gi, col0 in enumerate((2 * D, 5 * D)):
            k.dma('sp', bb[:, gi, :], b_ada[l:l + 1, col0:col0 + D].partition_broadcast(128), w=[('bb', gi)])
        for kind, fc0 in enumerate((0, 8, 24, 32)):
            for j in range(8):
                fc = fc0 + j
                wt, wk = wtile_load(wring, w_ada[l, :, fc * 128:(fc + 1) * 128], 128, q='sp', cast=False)
                ps, pk = psr.next()
                for kc in range(8):
                    mm(ps[:, 0:2], wt[:, kc, :], scT[:, kc::8], r=[wk, 'scT'], w=[pk], start=kc == 0, stop=kc == 7)
                act(modF[:, kind, j, :], ps[:, 0:2], AF.Identity, r=[pk, 'pT'], w=['modF'],
                    bias=P[:, R_BADA + fc:R_BADA + fc + 1])
        for which, (kind, rn) in enumerate(((1, R_N1), (3, R_N2))):
            for j in range(8):
                ts(a_sc[:, which, j, :], modF[:, kind, j, :], 1.0, ALU.add, r=['modF', 'pT'], w=['a_sc'],
                   s2=P[:, rn + j:rn + j + 1], op1=ALU.mult)
        for gi, col0 in enumerate((2 * D, 5 * D)):
            for j in range(8):
                wt, wk = wtile_load(wring, w_ada[l, :, col0 + j * 128:col0 + (j + 1) * 128], 128, q='sp', cast=False)
                for cond in range(2):
                    ps, pk = psr.next()
                    for kc in range(8):
                        mm(ps[:, 0:128], scB[:, cond * 8 + kc, :], wt[:, kc, :], r=[wk, 'scB'], w=[pk],
                           start=kc == 0, stop=kc == 7)
                    tt(gtb[:, gi, cond, j * 128:(j + 1) * 128], ps[:, 0:128], bb[:, gi, j * 128:(j + 1) * 128], ALU.add,
                       r=[pk, ('bb', gi)], w=['gtb'])
        ts(qgs, P[:, R_QG:R_QG + 1], 0.125, ALU.mult, r=['pT'], w=['qgs'])
        ts(ngb, P[:, R_GLAB:R_GLAB + 4], -1.0, ALU.mult, r=['pT'], w=['ngb'])
        k.dma('sp', dtb, dnab[l, 1:2, :].partition_broadcast(128), w=['dtb'])
        k.dma('sp', nexpA, dnab[l, 0:1, :].partition_broadcast(128), w=['nexpA'])
        act(nexpA, nexpA, AF.Exp, r=['nexpA'], w=['nexpA'])
        ts(nexpA, nexpA, -1.0, ALU.mult, r=['nexpA'], w=['nexpA'])
        k.dma('sp', glw, glaw[l].rearrange("z r c -> r z c"), w=['glw'])
        ph.close()

        def norm_to_hT(sc, G, xsrc, xkeyf, hT, which, ncols_extra=None):
            pass

        for G in (GP, GS):
            N, cond, t0 = G['N'], G['cond'], G['t0']
            nch = N // 128
            xsrc = G['xin'] if l == 0 else G['y']
            gs = Scope(k)
            hT = gs.sb("hT", [128, 8, N], MMDT)
            p1 = Scope(k)
            xring = Ring([p1.sb("x1_%d" % i, [128, D]) for i in range(3)], 'x1')
            xnring = Ring([p1.sb("xn_%d" % i, [128, D]) for i in range(2)], 'xn')
            junk = p1.sb("junk", [128, D]); ssq = p1.sb("ssq", [128, 4])
            for s in range(nch):
                xt, xk = xring.next()
                k.dma(QA, xt, xsrc[s * 128:(s + 1) * 128, :], r=[('y', G['name'], s)], w=[xk])
                sq = ssq[:, (s % 4):(s % 4) + 1]; sqk = ('ssq', s % 4)
                act(junk, xt, AF.Square, r=[xk], w=['junk', sqk], accum=sq)
                act(sq, sq, AF.Sqrt, r=[sqk], w=[sqk], scale=1.0 / D, bias=EPS)
                recip(sq, sq, r=[sqk], w=[sqk])
                xn, xnk = xnring.next()
                ts(xn, xt, sq, ALU.mult, r=[xk, sqk], w=[xnk])
                for half in range(2):
                    ps, pk = psr.next()
                    for j in range(4):
                        kc = half * 4 + j
                        tr(ps[:, j * 128:(j + 1) * 128], xn[:, kc * 128:(kc + 1) * 128], ident, r=[xnk], w=[pk])
                    for j in range(4):
                        kc = half * 4 + j
                        act(hT[:, kc, s * 128:(s + 1) * 128], ps[:, j * 128:(j + 1) * 128], AF.Identity,
                            r=[pk, 'a_sc', 'modF'], w=[('hT', kc, s)],
                            scale=a_sc[:, 0, kc, cond:cond + 1], bias=modF[:, 0, kc, cond:cond + 1])
            p1.close()

            def proj_fm(wring, col0, M, evac, nk_=8):
                wt, wk = wtile_load(wring, w_in[l, :, col0:col0 + M], M)
                for n0 in range(0, N, 512):
                    ps, pk = psr.next()
                    for kc in range(8):
                        mm(ps[0:M, :], wt[:, kc, 0:M], hT[:, kc, n0:n0 + 512], r=[wk, ('hT', kc)], w=[pk],
                           start=kc == 0, stop=kc == 7)
                    evac(ps, pk, n0)

            def proj_tm(wring, col0, M, evac):
                wt, wk = wtile_load(wring, w_in[l, :, col0:col0 + M], M)
                for s in range(nch):
                    ps, pk = psr.next()
                    for kc in range(8):
                        mm(ps[:, 0:M], hT[:, kc, s * 128:(s + 1) * 128], wt[:, kc, 0:M], r=[wk, ('hT', kc, s)], w=[pk],
                           start=kc == 0, stop=kc == 7)
                    evac(ps, pk, s)

            if STAGE >= 1 and not ((KDBG & 4) and G is GS) and not ((KDBG & 8) and G is GP):
                na = Scope(k)
                wring = Ring([na.sb("wA_%d" % i, [128, 8, 128], MMDT) for i in range(2)], 'wA')
                qT = na.sb("qT", [128, N], MMDT); kT = na.sb("kT", [128, N]); vtm = na.sb("vtm", [128, nch, 128])
                oT = na.sb("oT", [128, N], MMDT)
                vaug = [na.sb("vaug%d" % h_, [128, nch, 128], MMDT) for h_ in range(2)]
                for h_ in range(2):
                    mset(vaug[h_], 1.0, w=[('vaug', h_)])
                if BF:
                    kTb = na.sb("kTb", [128, N], MMDT)
                    ebring = Ring([na.sb("Eb_%d" % i, [128, 512]) for i in range(2)], 'Eb')
                else:
                    kTb = kT
                raw = na.sb("raw", [128, 512]); sqb = na.sb("sqb", [128, 512]); rsb = na.sb("rsb", [128, 512])
                ering = Ring([na.sb("E_%d" % i, [128, 512], MMDT) for i in range(5)], 'E')
                rd = na.sb("rd", [128, 512])
                ktm = Ring([na.sb("ktm_%d" % i, [128, 128]) for i in range(2)], 'ktm')
                if G is GS:
                    kcx = na.sb("kcx", [128, 512], MMDT); kcl = na.sb("kcl", [128, 4, 128]); vcx = na.sb("vcx", [128, 4, 128], MMDT)
                    Rt = na.sb("Rt", [128, 2, 1408])
                    vcxa = [na.sb("vcxa%d" % h_, [128, 4, 128], MMDT) for h_ in range(2)]
                    for h_ in range(2):
                        mset(vcxa[h_], 1.0, w=[('vcxa', h_)])
                for c in range(4):
                    def evac_qk(dst, dkey, gcol):
                        def f(ps, pk, n0):
                            act(raw, ps, AF.Copy, r=[pk], w=['raw'])
                            act(sqb, ps, AF.Square, r=[pk], w=['sqb'])
                            p2, pk2 = psr.next()
                            mm(p2, blk, sqb, r=['const', 'sqb'], w=[pk2])
                            act(rsb, p2, AF.Sqrt, r=[pk2], w=['rsb'], scale=1.0 / 64, bias=EPS)
                            recip(rsb, rsb, r=['rsb'], w=['rsb'])
                            stt(dst[:, n0:n0 + 512], raw, gcol, rsb, ALU.mult, ALU.mult, r=['raw', 'rsb', 'qgs', 'pT'],
                                w=[(dkey, n0 // 512)])
                            if BF and dkey == 'kT':
                                act(kTb[:, n0:n0 + 512], dst[:, n0:n0 + 512], AF.Copy, r=[(dkey, n0 // 512)], w=[('kTb', n0 // 512)])
                        return f
                    if not (KDBG & 16):
                        proj_fm(wring, C_NAQ + c * 128, 128, evac_qk(qT, 'qT', qgs[:, 0:1]))
                        proj_fm(wring, C_NAK + c * 128, 128, evac_qk(kT, 'kT', P[:, R_KG:R_KG + 1]))

                    def evac_v(ps, pk, s):
                        act(vtm[:, s, :], ps[:, 0:128], AF.Copy, r=[pk], w=[('vtm', s)])
                        cp(vaug[0][:, s, 0:64], vtm[:, s, 0:64], r=[('vtm', s)], w=[('vaug', 0, s)])
                        cp(vaug[1][:, s, 64:128], vtm[:, s, 64:128], r=[('vtm', s)], w=[('vaug', 1, s)], e=PE2)
                    if not (KDBG & 32):
                        proj_tm(wring, C_NAV + c * 128, 128, evac_v)
                    if G is GP:
                        for si, (q0, T) in enumerate(G['seqs'] if not (KDBG & 64) else []):
                            for j in range(T // 128):
                                s = q0 // 128 + j
                                k.dma(QA, nv[si, l, j * 128:(j + 1) * 128, c * 128:(c + 1) * 128], vtm[:, s, :],
                                      r=[('vtm', s)])
                                ps, pk = psr.next()
                                tr(ps[:, 0:128], kT[:, s * 128:(s + 1) * 128], ident, r=[('kT', s // 4)], w=[pk])
                                kt_, ktk = ktm.next()
                                cp(kt_, ps[:, 0:128], r=[pk], w=[ktk])
                                k.dma(QA, nk[si, l, j * 128:(j + 1) * 128, c * 128:(c + 1) * 128], kt_, r=[ktk])
                        for si, (q0, T) in enumerate(G['seqs'] if not (KDBG & 1) else []):
                            for hh in range(2):
                                pr = slice(64 * hh, 64 * hh + 64)
                                po = slice(64 * (1 - hh), 64 * (1 - hh) + 64)
                                O, Ok = PS[4 + (si * 2 + hh) % 4], PK[4 + (si * 2 + hh) % 4]
                                for kc2 in range(2):
                                    s = q0 // 128 + kc2
                                    ps, pk = psr.next()
                                    mm(ps[:, 0:T], kTb[pr, s * 128:(s + 1) * 128], qT[pr, q0:q0 + T],
                                       r=[('kT', s // 4), ('kTb', s // 4), ('qT', q0 // 512)], w=[pk])
                                    E, Ek = ering.next()
                                    act(E[:, 0:T], ps[:, 0:T], AF.Exp, r=[pk], w=[Ek])
                                    mm(O[:, 0:T], vaug[hh][:, s, :], E[:, 0:T], r=[('vaug', hh, s), Ek], w=[Ok], start=kc2 == 0, stop=kc2 == 1)
                                recip(rd[po, 0:T], O[po, 0:T], r=[Ok], w=['rd'])
                                tt(oT[pr, q0:q0 + T], O[pr, 0:T], rd[po, 0:T], ALU.mult, r=[Ok, 'rd'], w=[('oT', si, hh)])
                    else:
                        k.dma(QA, kcl, ck[l, :, c * 128:(c + 1) * 128].rearrange("(m p) f -> p m f", p=128), w=['kcl'])
                        load_w(vcx, 'vcx', cv[l, :, c * 128:(c + 1) * 128].rearrange("(m p) f -> p m f", p=128), 4, 128)
                        cp(vcxa[0][:, :, 0:64], vcx[:, :, 0:64], r=['vcx'], w=[('vcxa', 0)])
                        cp(vcxa[1][:, :, 64:128], vcx[:, :, 64:128], r=['vcx'], w=[('vcxa', 1)], e=PE2)
                        k.dma('sp', Rt, rtab[l, 2 * c:2 * c + 2].rearrange("h p f -> p h f"), w=['Rt'])
                        for m in range(4):
                            ps, pk = psr.next()
                            tr(ps[:, 0:128], kcl[:, m, :], ident, r=['kcl'], w=[pk])
                            cp(kcx[:, m * 128:(m + 1) * 128], ps[:, 0:128], r=[pk], w=['kcx'])
                        for hh in range(2 if not (KDBG & 2) else 0):
                            pr = slice(64 * hh, 64 * hh + 64)
                            po = slice(64 * (1 - hh), 64 * (1 - hh) + 64)
                            for g in range(4):
                                O, Ok = PS[4 + g], PK[4 + g]
                                qs = slice(g * 512, (g + 1) * 512)
                                rows = [8 * g + b for b in range(8)]
                                starts = [min(max(r_ - 4, 0), 24) for r_ in rows]
                                kr_lo = (min(starts) // 2) * 2
                                kr_hi = max(starts) + 7
                                chunks = [('ctx', m) for m in range(4)] + [('loc', kr0) for kr0 in range(kr_lo, kr_hi + 1, 2)]
                                def pv(vl, vk, E, Ek, first, last, c_lo, c_hi, O=O, Ok=Ok):
                                    mm(O[:, c_lo:c_hi], vl, E[:, 0:c_hi - c_lo], r=[vk, Ek], w=[Ok], start=first, stop=last)
                                pend = None
                                for ci, (kind, idx) in enumerate(chunks):
                                    first, last = ci == 0, ci == len(chunks) - 1
                                    ps, pk = psr.next()
                                    E, Ek = ering.next()
                                    if kind == 'loc':
                                        kr0 = idx
                                        s = kr0 // 2
                                        vbs = [[b for b in range(8) if starts[b] <= kr0 + a <= starts[b] + 7] for a in range(2)]
                                        allb = sorted(set(vbs[0] + vbs[1]))
                                        c_lo, c_hi = allb[0] * 64, (allb[-1] + 1) * 64
                                        W = c_hi - c_lo
                                        mm(ps[:, 0:W], kTb[pr, s * 128:(s + 1) * 128], qT[pr, g * 512 + c_lo:g * 512 + c_hi],
                                           r=[('kT', s // 4), ('kTb', s // 4), ('qT', g)], w=[pk])
                                        j0 = 10 - (kr0 - 8 * g)
                                        if BF:
                                            Eb, Ebk = ebring.next()
                                        else:
                                            Eb, Ebk = E, Ek
                                        tt(Eb[:, 0:W], ps[:, 0:W], Rt[:, hh, j0 * 64 + c_lo:j0 * 64 + c_hi], ALU.add, r=[pk, 'Rt'], w=[Ebk])
                                        for a in range(2):
                                            vb = vbs[a]
                                            pa = slice(64 * a, 64 * a + 64)
                                            if vb:
                                                lo, hi = vb[0] * 64 - c_lo, (vb[-1] + 1) * 64 - c_lo
                                                act(E[pa, lo:hi], Eb[pa, lo:hi], AF.Exp, r=[Ebk], w=[Ek])
                                                if lo > 0:
                                                    mset(E[pa, 0:lo], 0.0, w=[Ek])
                                                if hi < W:
                                                    mset(E[pa, hi:W], 0.0, w=[Ek])
                                            else:
                                                mset(E[pa, 0:W], 0.0, w=[Ek])
                                        vl, vk = vaug[hh][:, s, :], ('vaug', hh, s)
                                    else:
                                        m = idx
                                        mm(ps, kcx[pr, m * 128:(m + 1) * 128], qT[pr, qs], r=['kcx', ('qT', g)], w=[pk])
                                        act(E, ps, AF.Exp, r=[pk], w=[Ek])
                                        vl, vk = vcxa[hh][:, m, :], ('vcxa', hh)
                                        c_lo, c_hi = 0, 512
                                    if pend is not None:
                                        pv(*pend)
                                    pend = (vl, vk, E, Ek, first, last, c_lo, c_hi)
                                pv(*pend)
                                recip(rd[po, :], O[po, :], r=[Ok], w=['rd'])
                                tt(oT[pr, qs], O[pr, :], rd[po, :], ALU.mult, r=[Ok, 'rd'], w=[('oT', g, hh)])
                    if not (KDBG & 128):
                        k.dma(QA, OSC[0][c * 128:(c + 1) * 128, t0:t0 + N], oT, r=['oT'], w=[('osc', 0, G['name'], c)])
                na.close()

            if STAGE >= 2:
                dn = Scope(k)
                wring = Ring([dn.sb("wB_%d" % i, [128, 8, 128], MMDT) for i in range(2)], 'wB')
                Bt = dn.sb("Bt", [128, nch, 8]); NBt = dn.sb("NBt", [128, nch, 8]); Gt = dn.sb("Gt", [128, nch, 8])
                GC = dn.sb("GC", [128, nch, 8]); GL = dn.sb("GL", [128, nch, 8]); NGC = dn.sb("NGC", [128, nch, 8])
                EG = dn.sb("EG", [128, nch, 8]); EDL = dn.sb("EDL", [128, nch, 8]); EGL = dn.sb("EGL", [128, nch, 8])
                BEG = dn.sb("BEG", [128, nch, 8]); tmp8 = dn.sb("tmp8", [128, 8])

                def evac_bg(ps, pk, s):
                    act(Bt[:, s, :], ps[:, 0:8], AF.Sigmoid, r=[pk], w=['Bt', 'bgsync'])
                    tt(tmp8, ps[:, 8:16], dtb, ALU.add, r=[pk, 'dtb', 'bgsync'], w=['tmp8'])
                    act(tmp8, tmp8, AF.Exp, r=['tmp8'], w=['tmp8'])
                    act(tmp8, tmp8, AF.Ln, r=['tmp8'], w=['tmp8'], bias=1.0)
                    tt(Gt[:, s, :], tmp8, nexpA, ALU.mult, r=['tmp8', 'nexpA'], w=['Gt'])
                proj_tm(wring, C_DNB, 16, evac_bg)
                for s in range(nch):
                    for z in range(2):
                        ps, pk = psr.next()
                        mm(ps[:, 0:4], UCUM[z], Gt[:, s, 4 * z:4 * z + 4], r=['const', 'Gt'], w=[pk])
                        mm(ps[:, 4:8], ones, Gt[:, s, 4 * z:4 * z + 4], r=['const', 'Gt'], w=[pk])
                        cp(GC[:, s, 4 * z:4 * z + 4], ps[:, 0:4], r=[pk], w=['GC'])
                        cp(GL[:, s, 4 * z:4 * z + 4], ps[:, 4:8], r=[pk], w=['GL'])
                ts(NBt, Bt, -1.0, ALU.mult, r=['Bt'], w=['NBt'])
                ts(NGC, GC, -1.0, ALU.mult, r=['GC'], w=['NGC'])
                act(EG, GC, AF.Exp, r=['GC'], w=['EG'])
                act(EGL, GL, AF.Exp, r=['GL'], w=['EGL'])
                tt(EDL, GL, GC, ALU.subtract, r=['GL', 'GC'], w=['EDL'])
                act(EDL, EDL, AF.Exp, r=['EDL'], w=['EDL'])
                tt(BEG, Bt, EG, ALU.mult, r=['Bt', 'EG'], w=['BEG'])

                qTd = dn.sb("qTd", [128, N]); kTd = dn.sb("kTd", [128, N]); vTd = dn.sb("vTd", [128, N]); sgT = dn.sb("sgT", [128, N])
                ktm_ = dn.sb("ktmd", [128, nch, 128]); vtm_ = dn.sb("vtmd", [128, nch, 128]); oacc = dn.sb("oacc", [128, nch, 128])
                sq5 = dn.sb("sq5", [128, 512]); rs5 = dn.sb("rs5", [128, 512])
                Sst = [dn.sb("Sst%d" % z, [128, 128]) for z in range(2)]
                oTd = dn.sb("oTd", [128, N])
                raw2 = dn.sb("raw2", [128, N])
                rawbuf = [(oTd, 'oTd'), (raw2, 'raw2')]
                rawi = [0]
                oTb = dn.sb("oTb", [128, N], MMDT) if BF else oTd
                mring = {}
                for nm, cnt in (('Gb', 2), ('E1', 4), ('EmT', 2), ('EmS', 2), ('A', 4), ('Nn', 4), ('R', 4), ('attT', 4),
                                ('bv', 2), ('kbe', 2), ('kdec', 4), ('u', 4), ('wT', 4), ('vnew', 2), ('on', 2)):
                    mring[nm] = Ring([dn.sb("%s_%d" % (nm, i), [128, 128]) for i in range(cnt)], nm)
                ssd = dn.sb("ssd", [128, 4]); junkd = dn.sb("junkd", [128, 128])
                for hd in range(4):
                    def conv_silu(dst, dkey, jchunk):
                        rawd, rawk = rawbuf[rawi[0] % 2]

                        def f_all():
                            w0 = P[:, R_DNCONV + jchunk:R_DNCONV + jchunk + 1]
                            w1 = P[:, R_DNCONV + 12 + jchunk:R_DNCONV + 12 + jchunk + 1]
                            w2 = P[:, R_DNCONV + 24 + jchunk:R_DNCONV + 24 + jchunk + 1]
                            for (q0, T) in G['seqs']:
                                ts(dst[:, q0:q0 + T], rawd[:, q0:q0 + T], w1, ALU.mult, r=[rawk, 'pT'], w=[dkey])
                                stt(dst[:, q0 + 1:q0 + T], rawd[:, q0:q0 + T - 1], w0, dst[:, q0 + 1:q0 + T], ALU.mult, ALU.add,
                                    r=[rawk, 'pT', dkey], w=[dkey])
                                stt(dst[:, q0:q0 + T - 1], rawd[:, q0 + 1:q0 + T], w2, dst[:, q0:q0 + T - 1], ALU.mult, ALU.add,
                                    r=[rawk, 'pT', dkey], w=[dkey])
                            act(dst, dst, AF.Silu, r=[dkey], w=[dkey])
                        return f_all

                    def evac_raw(ps, pk, n0):
                        rawd, rawk = rawbuf[rawi[0] % 2]
                        act(rawd[:, n0:n0 + 512], ps, AF.Copy, r=[pk], w=[rawk])

                    def l2n(dst, dkey, const):
                        for n0 in range(0, N, 512):
                            act(sq5, dst[:, n0:n0 + 512], AF.Square, r=[dkey], w=['sq5'])
                            p2, pk2 = psr.next()
                            mm(p2, ones, sq5, r=['const', 'sq5'], w=[pk2])
                            act(rs5, p2, AF.Sqrt, r=[pk2], w=['rs5'], bias=EPS)
                            recip(rs5, rs5, r=['rs5'], w=['rs5'])
                            stt(dst[:, n0:n0 + 512], dst[:, n0:n0 + 512], const, rs5, ALU.mult, ALU.mult, r=[dkey, 'rs5'], w=[dkey])
                    proj_fm(wring, C_DNQ + hd * 128, 128, evac_raw); cq_ = conv_silu(qTd, 'qTd', hd); rawi[0] += 1
                    proj_fm(wring, C_DNK + hd * 128, 128, evac_raw); cq_(); l2n(qTd, 'qTd', 128 ** -0.5)
                    ck_ = conv_silu(kTd, 'kTd', 4 + hd); rawi[0] += 1
                    proj_fm(wring, C_DNV + hd * 128, 128, evac_raw); ck_(); l2n(kTd, 'kTd', 1.0)
                    conv_silu(vTd, 'vTd', 8 + hd)(); rawi[0] += 1

                    def evac_gate(ps, pk, n0):
                        act(sgT[:, n0:n0 + 512], ps, AF.Silu, r=[pk], w=['sgT'])
                    proj_fm(wring, C_DNG + hd * 128, 128, evac_gate)
                    for s in range(nch):
                        for (src, skey, dst, dkey) in ((kTd, 'kTd', ktm_, 'ktmd'), (vTd, 'vTd', vtm_, 'vtmd')):
                            ps, pk = psr.next()
                            tr(ps[:, 0:128], src[:, s * 128:(s + 1) * 128], ident, r=[skey], w=[pk])
                            act(dst[:, s, :], ps[:, 0:128], AF.Copy, r=[pk], w=[(dkey, s)])
                    mset(oacc, 0.0, w=['oacc'])

                    def unit_prep(z, s, U):
                        col = 4 * z + hd
                        c1 = slice(col, col + 1)
                        Gb, Gbk = mring['Gb'].next()
                        cp(Gb, Gt[:, s, c1].to_broadcast([128, 128]), r=['Gt'], w=[Gbk])
                        kc_ = kTd[:, s * 128:(s + 1) * 128]
                        ps1, pk1 = psr8.next()
                        mm(ps1[:, 0:128], Gb, UCUM[z], r=[Gbk, 'const'], w=[pk1])
                        ps2, pk2 = psr8.next()
                        mm(ps2[:, 0:128], kc_, kc_, r=['kTd'], w=[pk2])
                        yield
                        E1, E1k = mring['E1'].next()
                        act(E1, ps1[:, 0:128], AF.Exp, r=[pk1, 'NGC'], w=[E1k], bias=NGC[:, s, c1])
                        E2, E2k = mring['E1'].next()
                        act(E2, ps1[:, 0:128], AF.Exp, r=[pk1, 'GC'], w=[E2k], bias=GC[:, s, c1], scale=-1.0)
                        EmS, EmSk = mring['EmS'].next()
                        stt(EmS, E2, 1.0, MSTR[z], ALU.min, ALU.mult, r=[E2k, 'const'], w=[EmSk])
                        A0, Ak = mring['A'].next()
                        stt(A0, ps2[:, 0:128], NBt[:, s, c1], EmS, ALU.mult, ALU.mult, r=[pk2, 'NBt', EmSk], w=[Ak])
                        EmT, EmTk = mring['EmT'].next()
                        stt(EmT, E1, 1.0, MINC[z], ALU.min, ALU.mult, r=[E1k, 'const'], w=[EmTk])
                        ps3, pk3 = psr8.next()
                        tr(ps3[:, 0:128], A0, ident, r=[Ak], w=[pk3])
                        ps4, pk4 = psr8.next()
                        mm(ps4[:, 0:128], kc_, qTd[:, s * 128:(s + 1) * 128], r=['kTd', 'qTd'], w=[pk4])
                        yield
                        N0, Nk = mring['Nn'].next()
                        act(N0, ps3[:, 0:128], AF.Copy, r=[pk3], w=[Nk])
                        attT, attk = mring['attT'].next()
                        tt(attT, ps4[:, 0:128], EmT, ALU.mult, r=[pk4, EmTk], w=[attk])
                        Rm, Rk = mring['R'].next()
                        tt(Rm, N0, ident, ALU.add, r=[Nk, 'const'], w=[Rk], e=PE2)
                        Acur, Ack, Ncur, Nck = A0, Ak, N0, Nk
                        for lev in range(6):
                            pa, pak = psr8.next()
                            mm(pa[:, 0:128], Ncur, Acur, r=[Nck, Ack], w=[pak])
                            if lev < 5:
                                pn, pnk = psr8.next()
                                mm(pn[:, 0:128], Acur, Ncur, r=[Ack, Nck], w=[pnk])
                            yield
                            An, Ank = mring['A'].next()
                            act(An, pa[:, 0:128], AF.Copy, r=[pak], w=[Ank])
                            if lev < 5:
                                Nn, Nnk = mring['Nn'].next()
                                cp(Nn, pn[:, 0:128], r=[pnk], w=[Nnk])
                            prr, prk = psr8.next()
                            mm(prr[:, 0:128], An, Rm, r=[Ank, Rk], w=[prk])
                            yield
                            Rn, Rnk = mring['R'].next()
                            tt(Rn, prr[:, 0:128], Rm, ALU.add, r=[prk, Rk], w=[Rnk])
                            Rm, Rk = Rn, Rnk
                            Acur, Ack = An, Ank
                            if lev < 5:
                                Ncur, Nck = Nn, Nnk
                        bv, bvk = mring['bv'].next()
                        ts(bv, vtm_[:, s, :], Bt[:, s, c1], ALU.mult, r=[('vtmd', s), 'Bt'], w=[bvk])
                        kbe, kbek = mring['kbe'].next()
                        ts(kbe, ktm_[:, s, :], BEG[:, s, c1], ALU.mult, r=[('ktmd', s), 'BEG'], w=[kbek], e=PE2)
                        pu, puk = psr8.next()
                        mm(pu[:, 0:128], Rm, bv, r=[Rk, bvk], w=[puk])
                        pw, pwk = psr8.next()
                        mm(pw[:, 0:128], kbe, Rm, r=[kbek, Rk], w=[pwk])
                        yield
                        u, uk = mring['u'].next()
                        act(u, pu[:, 0:128], AF.Copy, r=[puk], w=[uk])
                        wT, wTk = mring['wT'].next()
                        act(wT, pw[:, 0:128], AF.Copy, r=[pwk], w=[wTk])
                        kdec, kdk = mring['kdec'].next()
                        ts(kdec, ktm_[:, s, :], EDL[:, s, c1], ALU.mult, r=[('ktmd', s), 'EDL'], w=[kdk], e=PE2)
                        U.update(u=(u, uk), wT=(wT, wTk), att=(attT, attk), kdec=(kdec, kdk))

                    def unit_step(z, s, U):
                        col = 4 * z + hd
                        c1 = slice(col, col + 1)
                        S_, Sk = Sst[z], ('Sst', z)
                        (u, uk), (wT, wTk), (attT, attk), (kdec, kdk) = U['u'], U['wT'], U['att'], U['kdec']
                        p1_, p1k = psr8.next()
                        mm(p1_[:, 0:128], wT, S_, r=[wTk, Sk], w=[p1k])
                        p2_, p2k = psr8.next()
                        mm(p2_[:, 0:128], qTd[:, s * 128:(s + 1) * 128], S_, r=['qTd', Sk], w=[p2k])
                        yield
                        vn, vnk = mring['vnew'].next()
                        tt(vn, u, p1_[:, 0:128], ALU.subtract, r=[uk, p1k], w=[vnk])
                        stt(oacc[:, s, :], p2_[:, 0:128], EG[:, s, c1], oacc[:, s, :], ALU.mult, ALU.add,
                            r=[p2k, 'EG', ('oacc', s)], w=[('oacc', s)])
                        p3_, p3k = psr8.next()
                        mm(p3_[:, 0:128], attT, vn, r=[attk, vnk], w=[p3k])
                        p4_, p4k = psr8.next()
                        mm(p4_[:, 0:128], kdec, vn, r=[kdk, vnk], w=[p4k])
                        yield
                        tt(oacc[:, s, :], p3_[:, 0:128], oacc[:, s, :], ALU.add, r=[p3k, ('oacc', s)], w=[('oacc', s)])
                        stt(S_, S_, EGL[:, s, c1], p4_[:, 0:128], ALU.mult, ALU.add, r=[Sk, 'EGL', p4k], w=[Sk])

                    for si, (q0, T) in enumerate(G['seqs']):
                        c0 = q0 // 128
                        ncs = T // 128
                        for z in range(2):
                            if G is GP:
                                mset(Sst[z], 0.0, w=[('Sst', z)])
                            else:
                                k.dma(QA, Sst[z], sdn[l, z, hd], w=[('Sst', z)])
                        Us = {}
                        for i in range(ncs + 1):
                            gens = []
                            if i < ncs:
                                Us[i] = ({}, {})
                                gens += [unit_prep(0, c0 + i, Us[i][0]), unit_prep(1, c0 + ncs - 1 - i, Us[i][1])]
                            if i >= 1:
                                gens += [unit_step(0, c0 + i - 1, Us[i - 1][0]), unit_step(1, c0 + ncs - i, Us[i - 1][1])]
                            run_rr(gens)
                        if G is GP:
                            for z in range(2):
                                k.dma(QA, ndn[si, l, z, hd], Sst[z], r=[('Sst', z)])
                    for s in range(nch):
                        sq = ssd[:, (s % 4):(s % 4) + 1]; sqk = ('ssd', s % 4)
                        act(junkd, oacc[:, s, :], AF.Square, r=[('oacc', s)], w=['junkd', sqk], accum=sq)
                        act(sq, sq, AF.Sqrt, r=[sqk], w=[sqk], scale=1.0 / 128, bias=EPS)
                        recip(sq, sq, r=[sqk], w=[sqk])
                        on, onk = mring['on'].next()
                        ts(on, oacc[:, s, :], sq, ALU.mult, r=[('oacc', s), sqk], w=[onk])
                        ps, pk = psr.next()
                        tr(ps[:, 0:128], on, ident, r=[onk], w=[pk])
                        stt(oTb[:, s * 128:(s + 1) * 128], ps[:, 0:128], P[:, R_DNG:R_DNG + 1], sgT[:, s * 128:(s + 1) * 128],
                            ALU.mult, ALU.mult, r=[pk, 'pT', 'sgT'], w=['oTd', 'oTb'])
                    k.dma(QA, OSC[1][hd * 128:(hd + 1) * 128, t0:t0 + N], oTb, r=['oTd', 'oTb'], w=[('osc', 1, G['name'], hd)])
                dn.close()

            if STAGE >= 3:
                gl = Scope(k)
                wring = Ring([gl.sb("wC_%d" % i, [128, 8, 128], MMDT) for i in range(2)], 'wC')
                wlr = gl.sb("wlr", [128, 8, 32], MMDT)
                lrt = gl.sb("lrt", [16, 512])
                qTg = gl.sb("qTg", [128, N]); kTg = gl.sb("kTg", [128, N])
                cs = gl.sb("cs", [128, N]); tmpN = gl.sb("tmpN", [128, N]); M0 = gl.sb("M0", [128, 512])
                qtl = gl.sb("qtl", [128, N]); ktl = gl.sb("ktl", [128, N]); kdT = gl.sb("kdT", [128, N])
                ebl = gl.sb("ebl", [128, nch]); tot = gl.sb("tot", [128, nch])
                vtg = [gl.sb("vtg%d" % h, [128, nch, 128]) for h in range(2)]
                sgg = gl.sb("sgg", [128, N])
                oag = [gl.sb("oag%d" % h, [128, nch, 128]) for h in range(2)]
                Sg = [gl.sb("Sg%d" % h, [128, 128]) for h in range(2)]
                oTg = gl.sb("oTgb", [128, N], MMDT) if BF else cs
                kdr = Ring([gl.sb("kd_%d" % i, [128, 128]) for i in range(2)], 'kd')
                atr = Ring([gl.sb("atg_%d" % i, [128, 128]) for i in range(3)], 'atg')
                onr = Ring([gl.sb("ong_%d" % i, [128, 128]) for i in range(2)], 'ong')
                ssg = gl.sb("ssg", [128, 4]); junkg = gl.sb("junkg", [128, 128])
                mset(M0, 1.0, w=['M0'])
                mset(M0.rearrange("p (c t) -> p c t", t=128)[:, :, 0:1], 0.0, w=['M0'])
                load_w(wlr, 'wlr', w_in[l, :, C_GLR:C_GLR + 32].rearrange("(kc p) m -> p kc m", p=128), 8, 32)
                for hh in range(2):
                    mset(Sg[hh], 0.0, w=[('Sg', hh)])
                for p2i in range(2):
                    def evq(ps, pk, n0):
                        act(qTg[:, n0:n0 + 512], ps, AF.Copy, r=[pk], w=['qTg'])

                    def evk(ps, pk, n0):
                        act(kTg[:, n0:n0 + 512], ps, AF.Copy, r=[pk], w=['kTg'])
                    proj_fm(wring, C_GQ + p2i * 128, 128, evq)
                    proj_fm(wring, C_GK + p2i * 128, 128, evk)
                    for hh in range(2):
                        h = 2 * p2i + hh

                        def evv(ps, pk, s, hh=hh):
                            act(vtg[hh][:, s, :], ps[:, 0:128], AF.Copy, r=[pk], w=[('vtg', hh, s)])
                        proj_tm(wring, C_GV + h * 128, 128, evv)
                    for z in range(2):
                        for n0 in range(0, N, 512):
                            pl, plk = psr.next()
                            for kc in range(8):
                                mm(pl[0:16, :], wlr[:, kc, 16 * z:16 * z + 16], hT[:, kc, n0:n0 + 512], r=['wlr', ('hT', kc)], w=[plk],
                                   start=kc == 0, stop=kc == 7)
                            act(lrt, pl[0:16, :], AF.Copy, r=[plk], w=['lrt'])
                            ps, pk = psr.next()
                            mm(ps, glw[:, z, p2i * 128:(p2i + 1) * 128], lrt, r=['glw', 'lrt'], w=[pk])
                            act(tmpN[:, n0:n0 + 512], ps, AF.Exp, r=[pk, 'ngb'], w=['tmpN'], scale=-1.0,
                                bias=ngb[:, 2 * z + p2i:2 * z + p2i + 1])
                        act(tmpN, tmpN, AF.Ln, r=['tmpN'], w=['tmpN'], bias=1.0)
                        for n0 in range(0, N, 512):
                            k.op('dve', lambda en: en.tensor_tensor_scan(out=cs[:, n0:n0 + 512], data0=M0, data1=tmpN[:, n0:n0 + 512],
                                                                         initial=0.0, op0=ALU.mult, op1=ALU.add),
                                 r=['M0', 'tmpN'], w=['cs'])
                        cs3 = cs.rearrange("p (c t) -> p c t", t=128)
                        tm3 = tmpN.rearrange("p (c t) -> p c t", t=128)
                        if z == 1:
                            cp(tot, cs3[:, :, 127], r=['cs'], w=['tot'])
                            tt(tmpN, tmpN, cs, ALU.subtract, r=['tmpN', 'cs'], w=['tmpN'])
                            tt(cs3, tm3, tot.to_broadcast([128, nch, 128]) if False else tot[:, :, None].to_broadcast([128, nch, 128]),
                               ALU.add, r=['tmpN', 'tot'], w=['cs'])
                        lastc = 127 if z == 0 else 0
                        cp(tot, cs3[:, :, lastc], r=['cs'], w=['tot'])
                        act(qtl, cs, AF.Exp, r=['cs'], w=['qtl'], scale=-1.0 / 16)
                        stt(qtl, qtl, 0.125, qTg, ALU.mult, ALU.mult, r=['qtl', 'qTg'], w=['qtl'])
                        act(ktl, cs, AF.Exp, r=['cs'], w=['ktl'], scale=1.0 / 16)
                        tt(ktl, ktl, kTg, ALU.mult, r=['ktl', 'kTg'], w=['ktl'])
                        kd3 = kdT.rearrange("p (c t) -> p c t", t=128)
                        tt(kd3, cs3, tot[:, :, None].to_broadcast([128, nch, 128]), ALU.subtract, r=['cs', 'tot'], w=['kdT'])
                        act(kdT, kdT, AF.Exp, r=['kdT'], w=['kdT'], scale=1.0 / 16)
                        tt(kdT, kdT, kTg, ALU.mult, r=['kdT', 'kTg'], w=['kdT'])
                        act(ebl, tot, AF.Exp, r=['tot'], w=['ebl'], scale=-1.0 / 16)
                        for si, (q0, T) in enumerate(G['seqs']):
                            c0, ncs = q0 // 128, T // 128
                            for hh in range(2):
                                h = 2 * p2i + hh
                                prs = slice(64 * hh, 64 * hh + 64)
                                if G is GP:
                                    mset(Sg[hh][prs, :], 0.0, w=[('Sg', hh)])
                                else:
                                    k.dma(QA, Sg[hh][prs, :], sgla[l, z, h], w=[('Sg', hh)])
                            for i in range(ncs):
                                s = c0 + i if z == 0 else c0 + ncs - 1 - i
                                sl = slice(s * 128, (s + 1) * 128)
                                ps, pk = psr.next()
                                tr(ps[:, 0:128], kdT[:, sl], ident, r=['kdT'], w=[pk])
                                kd, kdk = kdr.next()
                                act(kd, ps[:, 0:128], AF.Copy, r=[pk], w=[kdk])
                                for hh in range(2):
                                    prs = slice(64 * hh, 64 * hh + 64)
                                    pa, pak = psr.next()
                                    mm(pa[:, 0:128], ktl[prs, sl], qtl[prs, sl], r=['ktl', 'qtl'], w=[pak])
                                    at, atk = atr.next()
                                    tt(at, pa[:, 0:128], MINC[z], ALU.mult, r=[pak, 'const'], w=[atk])
                                    O, Ok = PS[4 + hh], PK[4 + hh]
                                    mm(O[:, 0:128], at, vtg[hh][:, s, :], r=[atk, ('vtg', hh, s)], w=[Ok], start=True, stop=False)
                                    mm(O[:, 0:128], qtl[:, sl], Sg[hh], r=['qtl', ('Sg', hh)], w=[Ok], start=False, stop=True)
                                    if z == 0:
                                        act(oag[hh][:, s, :], O[:, 0:128], AF.Copy, r=[Ok], w=[('oag', hh, s)])
                                    else:
                                        tt(oag[hh][:, s, :], O[:, 0:128], oag[hh][:, s, :], ALU.add, r=[Ok, ('oag', hh, s)], w=[('oag', hh, s)])
                                    S2, S2k = PS[6 + hh], PK[6 + hh]
                                    mm(S2[:, 0:128], kd, vtg[hh][:, s, :], r=[kdk, ('vtg', hh, s)], w=[S2k])
                                    stt(Sg[hh][prs, :], Sg[hh][prs, :], ebl[prs, s:s + 1], S2[prs, 0:128], ALU.mult, ALU.add,
                                        r=[('Sg', hh), 'ebl', S2k], w=[('Sg', hh)])
                            if G is GP:
                                for hh in range(2):
                                    h = 2 * p2i + hh
                                    k.dma(QA, ngla[si, l, z, h], Sg[hh][64 * hh:64 * hh + 64, :], r=[('Sg', hh)])
                    for hh in range(2):
                        h = 2 * p2i + hh

                        def evg(ps, pk, n0):
                            act(sgg[:, n0:n0 + 512], ps, AF.Silu, r=[pk], w=['sgg'])
                        proj_fm(wring, C_GG + h * 128, 128, evg)
                        for s in range(nch):
                            sq = ssg[:, (s % 4):(s % 4) + 1]; sqk = ('ssg', s % 4)
                            act(junkg, oag[hh][:, s, :], AF.Square, r=[('oag', hh, s)], w=['junkg', sqk], accum=sq)
                            act(sq, sq, AF.Sqrt, r=[sqk], w=[sqk], scale=1.0 / 128, bias=EPS)
                            recip(sq, sq, r=[sqk], w=[sqk])
                            on, onk = onr.next()
                            ts(on, oag[hh][:, s, :], sq, ALU.mult, r=[('oag', hh, s), sqk], w=[onk])
                            ps, pk = psr.next()
                            tr(ps[:, 0:128], on, ident, r=[onk], w=[pk])
                            stt(oTg[:, s * 128:(s + 1) * 128], ps[:, 0:128], P[:, R_GLAG:R_GLAG + 1], sgg[:, s * 128:(s + 1) * 128],
                                ALU.mult, ALU.mult, r=[pk, 'pT', 'sgg'], w=['cs', 'oTgb'])
                        k.dma(QA, OSC[2][h * 128:(h + 1) * 128, t0:t0 + N], oTg, r=['cs', 'oTgb'], w=[('osc', 2, G['name'], h)])
                gl.close()

            if STAGE >= 4:
                mg = Scope(k)
                wbr_r = Ring([mg.sb("wbr_%d" % i, [128, 4, 128], MMDT) for i in range(3)], 'wbr')
                wm_r = Ring([mg.sb("wm_%d" % i, [128, 8, 128], MMDT) for i in range(3)], 'wm')
                wo = [mg.sb("wo_%d" % i, [128, 8, 512], MMDT) for i in range(2)]
                ob = [mg.sb("ob_%d" % i, [128, 4, 512], MMDT) for i in range(3)]
                mT = mg.sb("mT", [128, 8, 512], MMDT)
                mTf = mg.sb("mTf", [128, 8, 512]) if BF else mT
                sgr = Ring([mg.sb("sg_%d" % i, [128, 512]) for i in range(2)], 'sg')
                tmr = Ring([mg.sb("tm_%d" % i, [128, 512]) for i in range(2)], 'tm')
                xr = Ring([mg.sb("xm_%d" % i, [128, D]) for i in range(2)], 'xm')
                xo_r = Ring([mg.sb("xo_%d" % i, [128, D]) for i in range(2)], 'xo')
                for n0 in range(0, N, 512):
                    for br in range(3):
                        k.dma(QA, ob[br], OSC[br][:, t0 + n0:t0 + n0 + 512].rearrange("(kc p) t -> p kc t", p=128),
                              r=[('osc', br, G['name'])], w=[('ob', br)])
                    for half in range(2):
                        if not BF or SWCAST:
                            k.dma(QW, wo[half], w_out[l, :, half * 512:(half + 1) * 512].rearrange("(kc p) m -> p kc m", p=128),
                                  w=[('wo', half)])
                            continue
                        for q4 in range(4):
                            c0_ = half * 512 + q4 * 128
                            load_w(wo[half][:, :, q4 * 128:(q4 + 1) * 128], ('wo', half, q4),
                                   w_out[l, :, c0_:c0_ + 128].rearrange("(kc p) m -> p kc m", p=128), 8, 128)
                    for oc in range(8):
                        for br in range(3):
                            wb, wbk = wbr_r.next()
                            load_w(wb, wbk, w_br[l, br, :, oc * 128:(oc + 1) * 128].rearrange("(kc p) m -> p kc m", p=128), 4, 128)
                            pp, ppk = psr.next()
                            for kc in range(4):
                                mm(pp, wb[:, kc, :], ob[br][:, kc, :], r=[wbk, ('ob', br)], w=[ppk], start=kc == 0, stop=kc == 3)
                            wt, wk = wtile_load(wm_r, w_in[l, :, C_M[br] + oc * 128:C_M[br] + (oc + 1) * 128], 128)
                            pm, pmk = psr.next()
                            for kc in range(8):
                                mm(pm, wt[:, kc, :], hT[:, kc, n0:n0 + 512], r=[wk, ('hT', kc)], w=[pmk], start=kc == 0, stop=kc == 7)
                            sg, sgk = sgr.next()
                            act(sg, pm, AF.Sigmoid, r=[pmk], w=[sgk])
                            if br == 0:
                                tt(mTf[:, oc, :], pp, sg, ALU.mult, r=[ppk, sgk], w=[('mTf', oc)])
                            else:
                                tm, tmk = tmr.next()
                                tt(tm, pp, sg, ALU.mult, r=[ppk, sgk], w=[tmk])
                                dst = mT if br == 2 else mTf
                                tt(dst[:, oc, :], mTf[:, oc, :], tm, ALU.add, r=[('mTf', oc), tmk], w=[('mTf', oc), ('mT', oc)], e=PE2)
                    for sub in range(4):
                        s = n0 // 128 + sub
                        xt, xk = xr.next()
                        k.dma(QA, xt, xsrc[s * 128:(s + 1) * 128, :], r=[('y', G['name'], s)], w=[xk])
                        xo, xok = xo_r.next()
                        for half in range(2):
                            pp, ppk = psr.next()
                            for kc in range(8):
                                mm(pp, mT[:, kc, sub * 128:(sub + 1) * 128], wo[half][:, kc, :], r=[('mT', kc), ('mTf', kc), ('wo', half)], w=[ppk],
                                   start=kc == 0, stop=kc == 7)
                            tt(xo[:, half * 512:(half + 1) * 512], pp, gtb[:, 0, cond, half * 512:(half + 1) * 512], ALU.mult,
                               r=[ppk, 'gtb'], w=[(xok[0], xok[1], half)])
                        tt(xo, xo, xt, ALU.add, r=[xok, xk], w=[xok], e=PE2)
                        k.dma(QA, G['y'][s * 128:(s + 1) * 128, :], xo, r=[xok], w=[('y', G['name'], s)])
                mg.close()
            gs.close()

        if STAGE >= 5:
            ff = Scope(k)
            wu_r = Ring([ff.sb("wu_%d" % i, [128, 8, 128], MMDT) for i in range(4)], 'wu')
            wd_r = Ring([ff.sb("wd_%d" % i, [128, D], MMDT) for i in range(4)], 'wd')
            h2 = ff.sb("h2", [128, 8, 514], MMDT)
            actT = ff.sb("actT", [128, 22, 512], MMDT)
            xm = ff.sb("xmf", [128, 4, D]); xh_all = ff.sb("xh", [2, 4, D]); xhn = ff.sb("xhn", [2, D])
            xn_r = Ring([ff.sb("xnf_%d" % i, [128, D]) for i in range(2)], 'xnf')
            junkf = ff.sb("junkf", [128, D]); ssf = ff.sb("ssf", [128, 4]); ssh = ff.sb("ssh", [2, 1])
            cu_r = Ring([ff.sb("cu_%d" % i, [128, 512]) for i in range(2)], 'cu')
            sgf = ff.sb("sgf", [128, 512])
            xo_r = Ring([ff.sb("xof_%d" % i, [128, D]) for i in range(2)], 'xof')
            tiles = []
            last = (l == 1)
            for G in (GP, GS):
                for n0 in range(0, G['N'], 512):
                    if G is GP:
                        tiles.append((G, n0, [(0, 256), (256, 256)], False, False, False))
                    elif not last:
                        tiles.append((G, n0, [(0, 512)], n0 > 0, n0 + 512 < G['N'], False))
                    elif n0 == 0:
                        tiles.append((G, n0, [(0, 512)], True, True, True))
            if not last:
                for ti in range(4):
                    n0 = 512 * ti
                    lrow = n0 - 1 if ti > 0 else n0
                    rrow = n0 + 512 if ti < 3 else n0
                    k.dma(QA, xh_all[0:1, ti, :], ys[lrow:lrow + 1, :], r=[('y', 'S', lrow // 128)], w=[('xh', ti)])
                    k.dma(QA, xh_all[1:2, ti, :], ys[rrow:rrow + 1, :], r=[('y', 'S', rrow // 128)], w=[('xh', ti)])
            else:
                w0h = ff.sb("w0h", [128, 44]); w2h = ff.sb("w2h", [128, 44])
                ts(w0h, P[:, R_FCONV:R_FCONV + 44], selc[:, 4:5], ALU.mult, r=['pT', 'selc'], w=['w0h'])
                ts(w2h, P[:, R_FCONV + 88:R_FCONV + 132], selc[:, 5:6], ALU.mult, r=['pT', 'selc'], w=['w2h'])
                for i in range(3):
                    k.dma(QA, xh_all[0:1, i, :], ys[(i + 1) * 512 - 1:(i + 1) * 512, :], r=[('y', 'S', (i + 1) * 4 - 1)], w=[('xh', i)])
                    k.dma(QA, xh_all[1:2, i, :], ys[(i + 1) * 512:(i + 1) * 512 + 1, :], r=[('y', 'S', (i + 1) * 4)], w=[('xh', i)])
                ts(xh_all[:, 3, :], xh_all[:, 0, :], selc[0:2, 8:9], ALU.mult, r=[('xh', 0), 'selc'], w=[('xh', 3)])
                for i in (1, 2):
                    stt(xh_all[:, 3, :], xh_all[:, i, :], selc[0:2, 8 + i:9 + i], xh_all[:, 3, :], ALU.mult, ALU.add,
                        r=[('xh', i), ('xh', 3), 'selc'], w=[('xh', 3)])
            for (G, n0, segs, hl, hr, own) in tiles:
                cond = G['cond']
                yv = G['y']
                xh = xh_all[:, 3 if own else n0 // 512, :]
                xhk = ('xh', 3 if own else n0 // 512)
                if not own:
                    k.dma(QA, xm, yv[n0:n0 + 512, :].rearrange("(s p) d -> p s d", p=128),
                          r=[('y', G['name'], n0 // 128 + i) for i in range(4)], w=['xmf'])
                else:
                    for sub in range(4):
                        for r_ in range(4):
                            xb_, xbk = xo_r.next()
                            row0 = r_ * 512 + sub * 128
                            k.dma(QA, xb_, ys[row0:row0 + 128, :], r=[('y', 'S', row0 // 128)], w=[xbk])
                            if r_ == 0:
                                ts(xm[:, sub, :], xb_, selc[:, 0:1], ALU.mult, r=[xbk, 'selc'], w=[('xmf', sub)])
                            else:
                                stt(xm[:, sub, :], xb_, selc[:, r_:r_ + 1], xm[:, sub, :], ALU.mult, ALU.add,
                                    r=[xbk, 'selc', ('xmf', sub)], w=[('xmf', sub)])
                halo = hl or hr
                if halo:
                    act(junkf[0:2, :], xh, AF.Square, r=[xhk], w=['junkf', 'ssh'], accum=ssh)
                    act(ssh, ssh, AF.Sqrt, r=['ssh'], w=['ssh'], scale=1.0 / D, bias=EPS)
                    recip(ssh, ssh, r=['ssh'], w=['ssh'])
                    ts(xhn, xh, ssh, ALU.mult, r=[xhk, 'ssh'], w=['xhn'])
                    for half in range(2):
                        ps, pk = psr.next()
                        for j in range(4):
                            kc = half * 4 + j
                            tr(ps[:, j * 2:j * 2 + 2], xhn[0:2, kc * 128:(kc + 1) * 128], ident[0:2, 0:2], r=['xhn'], w=[pk])
                        for j in range(4):
                            kc = half * 4 + j
                            act(h2[:, kc, 512:514], ps[:, j * 2:j * 2 + 2], AF.Identity, r=[pk, 'a_sc', 'modF'], w=[('h2', kc, 'h')],
                                scale=a_sc[:, 1, kc, cond:cond + 1], bias=modF[:, 2, kc, cond:cond + 1])
                for sub in range(4):
                    sq = ssf[:, sub:sub + 1]; sqk = ('ssf', sub)
                    act(junkf, xm[:, sub, :], AF.Square, r=['xmf'], w=['junkf', sqk], accum=sq)
                    act(sq, sq, AF.Sqrt, r=[sqk], w=[sqk], scale=1.0 / D, bias=EPS)
                    recip(sq, sq, r=[sqk], w=[sqk])
                    xn, xnk = xn_r.next()
                    ts(xn, xm[:, sub, :], sq, ALU.mult, r=['xmf', sqk], w=[xnk])
                    for half in range(2):
                        ps, pk = psr.next()
                        for j in range(4):
                            kc = half * 4 + j
                            tr(ps[:, j * 128:(j + 1) * 128], xn[:, kc * 128:(kc + 1) * 128], ident, r=[xnk], w=[pk])
                        for j in range(4):
                            kc = half * 4 + j
                            act(h2[:, kc, sub * 128:(sub + 1) * 128], ps[:, j * 128:(j + 1) * 128], AF.Identity,
                                r=[pk, 'a_sc', 'modF'], w=[('h2', kc, sub)],
                                scale=a_sc[:, 1, kc, cond:cond + 1], bias=modF[:, 2, kc, cond:cond + 1])

                def up_chunk(j):
                    wt, wk = wtile_load(wu_r, w_up[l, :, j * 128:(j + 1) * 128], 128)
                    pu, puk = psr.next()
                    for kc in range(8):
                        mm(pu, wt[:, kc, :], h2[:, kc, 0:512], r=[wk, ('h2', kc)], w=[puk], start=kc == 0, stop=kc == 7)
                    if halo:
                        phh, phk = psr.next()
                        for kc in range(8):
                            mm(phh[:, 0:2], wt[:, kc, :], h2[:, kc, 512:514], r=[wk, ('h2', kc)], w=[phk], start=kc == 0, stop=kc == 7)
                    w0 = P[:, R_FCONV + j:R_FCONV + j + 1]
                    w1 = P[:, R_FCONV + 44 + j:R_FCONV + 44 + j + 1]
                    w2 = P[:, R_FCONV + 88 + j:R_FCONV + 88 + j + 1]
                    cu, cuk = cu_r.next()
                    act(cu, pu, AF.Identity, r=[puk, 'pT'], w=[cuk], scale=w1)
                    for (q0, T) in segs:
                        stt(cu[:, q0 + 1:q0 + T], pu[:, q0:q0 + T - 1], w0, cu[:, q0 + 1:q0 + T], ALU.mult, ALU.add, r=[puk, 'pT', cuk], w=[cuk])
                        stt(cu[:, q0:q0 + T - 1], pu[:, q0 + 1:q0 + T], w2, cu[:, q0:q0 + T - 1], ALU.mult, ALU.add, r=[puk, 'pT', cuk], w=[cuk])
                    w0e = w0h[:, j:j + 1] if own else w0
                    w2e = w2h[:, j:j + 1] if own else w2
                    if hl:
                        stt(cu[:, 0:1], phh[:, 0:1], w0e, cu[:, 0:1], ALU.mult, ALU.add, r=[phk, 'pT', 'w0h', cuk], w=[cuk])
                    if hr:
                        stt(cu[:, 511:512], phh[:, 1:2], w2e, cu[:, 511:512], ALU.mult, ALU.add, r=[phk, 'pT', 'w2h', cuk], w=[cuk])
                    return cu, cuk
                for jj in range(22):
                    cu, cuk = up_chunk(22 + jj)
                    act(sgf, cu, AF.Silu, r=[cuk], w=['sgf'])
                    cu, cuk = up_chunk(jj)
                    tt(actT[:, jj, :], cu, sgf, ALU.mult, r=[cuk, 'sgf'], w=[('actT', jj)], e=PE2)
                for jj in range(22):
                    wd, wdk = wd_r.next()
                    load_w(wd.rearrange("p (a b) -> p a b", b=128), wdk, w_dn[l, jj * 128:(jj + 1) * 128, :].rearrange("p (a b) -> p a b", b=128), 8, 128)
                    for sub in range(4):
                        for half in range(2):
                            b = sub * 2 + half
                            mm(PS[b], actT[:, jj, sub * 128:(sub + 1) * 128], wd[:, half * 512:(half + 1) * 512],
                               r=[('actT', jj), wdk], w=[PK[b]], start=jj == 0, stop=jj == 21)
                for sub in range(4):
                    s = n0 // 128 + sub
                    xo, xok = xo_r.next()
                    for half in range(2):
                        b = sub * 2 + half
                        tt(xo[:, half * 512:(half + 1) * 512], PS[b], gtb[:, 1, cond, half * 512:(half + 1) * 512], ALU.mult,
                           r=[PK[b], 'gtb'], w=[(xok[0], xok[1], half)])
                    tt(xo, xo, xm[:, sub, :], ALU.add, r=[xok, 'xmf'], w=[xok], e=PE2)
                    if own:
                        k.dma(QA, ys_own[sub * 128:(sub + 1) * 128, :], xo, r=[xok], w=[('yown', sub)])
                    else:
                        k.dma(QA, yv[s * 128:(s + 1) * 128, :], xo, r=[xok], w=[('y', G['name'], s)])
            ff.close()
        lay.close()
    top.close()
    return nc


def _consts():
    i = np.arange(128)
    ident = np.eye(128, dtype=np.float32)
    ones = np.ones((128, 128), np.float32)
    blk = (i[:, None] // 64 == i[None, :] // 64).astype(np.float32)
    tu = (i[:, None] <= i[None, :]).astype(np.float32)
    tl = (i[:, None] >= i[None, :]).astype(np.float32)
    su = (i[:, None] < i[None, :]).astype(np.float32)
    sl = (i[:, None] > i[None, :]).astype(np.float32)
    return np.ascontiguousarray(np.concatenate([ident, ones, blk, tu, tl, su, sl], axis=1))


def _rtab(na_rpb):
    p = np.arange(128)
    a, kc = p // 64, p % 64
    jp = np.arange(22)
    qc = np.arange(64)
    delta = a[:, None] + 10 - jp[None, :]
    dvalid = np.abs(delta) <= 7
    dr = np.clip(delta + 7, 0, 14)
    dcol = kc[:, None] - qc[None, :]
    ws = np.clip(qc - 8, 0, 48)
    cvalid = (kc[:, None] >= ws[None, :]) & (kc[:, None] < ws[None, :] + 16)
    dc = np.clip(dcol, -15, 15) + 15
    g = na_rpb[:, :, dr[:, :, None], dc[:, None, :]]
    out = np.where(cvalid[None, None, :, None, :], g, np.float32(NEG))
    out = np.where(dvalid[None, None, :, :, None], out, np.float32(0.0))
    return np.ascontiguousarray(out.reshape(2, 8, 128, 22 * 64).astype(np.float32))


def _pack_params(norm1, norm2, b_ada, dn_conv, ffn_conv, qg, kg, dng, glag, glab):
    pf = np.zeros((2, 256, 128), np.float32)
    for l in range(2):
        pf[l, R_N1:R_N1 + 8] = norm1[l].reshape(8, 128)
        pf[l, R_N2:R_N2 + 8] = norm2[l].reshape(8, 128)
        pf[l, R_BADA:R_BADA + 48] = b_ada[l].reshape(48, 128)
        pf[l, R_DNCONV:R_DNCONV + 36] = dn_conv[l].reshape(36, 128)
        pf[l, R_FCONV:R_FCONV + 132] = ffn_conv[l].reshape(132, 128)
        pf[l, R_QG] = np.tile(qg[l], 2)
        pf[l, R_KG] = np.tile(kg[l], 2)
        pf[l, R_DNG] = dng[l]
        pf[l, R_GLAG] = glag[l]
        pf[l, R_GLAB:R_GLAB + 4] = glab[l].reshape(4, 128)
    return pf


_PROG = None


def kernel(x_prompt, x_sample, cache_k, cache_v, state_dn, state_gla, c, c_ctx,
           w_ada, b_ada, norm1, w_in, na_q_norm, na_k_norm, na_rpb,
           dn_conv, dn_a_log, dn_dt_bias, dn_out_norm,
           gla_w_gate, gla_b_gate, gla_out_norm,
           w_branch, w_out, norm2, w_up, ffn_conv, w_down):
    global _PROG
    f = lambda a: np.ascontiguousarray(np.asarray(a, dtype=np.float32))
    x_prompt, x_sample = f(x_prompt), f(x_sample)
    if _PROG is None:
        _PROG = build_program()
    nc = _PROG
    pf = _pack_params(f(norm1), f(norm2), f(b_ada), f(dn_conv), f(ffn_conv), f(na_q_norm), f(na_k_norm),
                      f(dn_out_norm), f(gla_out_norm), f(gla_b_gate))
    rt = _rtab(f(na_rpb))
    dnab = np.ascontiguousarray(np.stack([f(dn_a_log).reshape(2, 8), f(dn_dt_bias).reshape(2, 8)], axis=1))
    consts = _consts()
    shared = dict(w_ada=f(w_ada), b_ada=f(b_ada), w_in=f(w_in), w_br=f(w_branch), w_out=f(w_out), w_up=f(w_up),
                  w_dn=f(w_down), pf=pf, rtab=rt, dnab=dnab, glaw=f(gla_w_gate), consts=consts)
    in_maps = []
    for core in range(8):
        sb = core // 4
        m = dict(shared)
        m['xp'] = np.ascontiguousarray(x_prompt[4 * core:4 * core + 4].reshape(NP_TOK, D))
        m['xs'] = np.ascontiguousarray(x_sample[sb])
        m['ck'] = np.ascontiguousarray(f(cache_k)[sb].reshape(2, 512, 512))
        m['cv'] = np.ascontiguousarray(f(cache_v)[sb].reshape(2, 512, 512))
        m['sdn'] = np.ascontiguousarray(f(state_dn)[sb])
        m['sgla'] = np.ascontiguousarray(f(state_gla)[sb])
        m['cvec'] = np.ascontiguousarray(np.concatenate([f(c_ctx).reshape(8, 128), f(c)[sb].reshape(8, 128)], axis=0))
        rk = core % 4
        selc = np.zeros((128, 16), np.float32)
        selc[:, rk] = 1.0
        selc[:, 4] = 0.0 if rk == 0 else 1.0
        selc[:, 5] = 0.0 if rk == 3 else 1.0
        for i in range(3):
            selc[0, 8 + i] = 1.0 if rk == i + 1 else 0.0
            selc[1, 8 + i] = 1.0 if rk == i else 0.0
        m['selc'] = selc
        in_maps.append(m)
    res = run_bass_kernel_spmd(nc, in_maps, core_ids=list(range(8)))
    R = res.results
    y_p = np.concatenate([R[i]['yp'].reshape(4, 256, D) for i in range(8)], axis=0)
    y_s = np.stack([np.concatenate([R[4 * b + j]['ys_own'] for j in range(4)], axis=0) for b in range(2)], axis=0)
    nk_ = np.concatenate([R[i]['nk'].reshape(4, 2, 256, 8, 64) for i in range(8)], axis=0)
    nv_ = np.concatenate([R[i]['nv'].reshape(4, 2, 256, 8, 64) for i in range(8)], axis=0)
    ndn_ = np.concatenate([R[i]['ndn'] for i in range(8)], axis=0)
    ngla_ = np.concatenate([R[i]['ngla'] for i in range(8)], axis=0)
    return (y_p.astype(np.float32), y_s.astype(np.float32), nk_.astype(np.float32), nv_.astype(np.float32),
            ndn_.astype(np.float32), ngla_.astype(np.float32))
```
